# Optimizing a Trainium2 kernel written in Bass

```python
import math
import jax, jax.numpy as jnp
from jax import lax
import numpy as np

D_MODEL = 1024
BATCH = 32
SEQ = 256
DEPTH = 2
DEC_BATCH = 8
DEC_SEQ = 2048
PAST_LEN = 512

GRID_W = 64
N_MIXERS = 2
N_CONV_LAYERS = (DEPTH + 1) // 2
N_ATTN_LAYERS = DEPTH // 2
HEAD_DIM = 128
N_HEADS = D_MODEL // HEAD_DIM
N_KV_HEADS = 2
QKV_DIM = (N_HEADS + 2 * N_KV_HEADS) * HEAD_DIM
D_FF = 2816
CONV_WIDTH = 3
Q_BLOCK = 128
ROPE_THETA = 10000.0
EPS = 1e-6

kernel_name = "hybrid_diffusion_conv_gqa_step"


def rms_norm(x, g):
    xf = x.astype(jnp.float32)
    y = xf * lax.rsqrt(jnp.mean(xf * xf, axis=-1, keepdims=True) + EPS)
    return (y * g.astype(jnp.float32)).astype(x.dtype)


def dwconv3(x, w):
    xp = jnp.pad(x, ((0, 0), (1, 1), (0, 0)))
    return xp[:, :-2] * w[0] + xp[:, 1:-1] * w[1] + xp[:, 2:] * w[2]


def short_conv_mixer(u, w_in, conv_k, w_out):
    b_gate, c_gate, xp = jnp.split(u @ w_in, 3, axis=-1)
    return (b_gate * dwconv3(c_gate * xp, conv_k)) @ w_out


def conv_ffn(u, w_up, conv_k, w_down):
    h = dwconv3(u @ w_up, conv_k)
    g, up = jnp.split(h, 2, axis=-1)
    return (jax.nn.silu(g) * up) @ w_down


def qkv_heads(u, w_qkv, q_gain, k_gain):
    b, l, _ = u.shape
    qkv = u @ w_qkv
    q = qkv[..., :N_HEADS * HEAD_DIM].reshape(b, l, N_HEADS, HEAD_DIM)
    k = qkv[..., N_HEADS * HEAD_DIM:(N_HEADS + N_KV_HEADS) * HEAD_DIM].reshape(b, l, N_KV_HEADS, HEAD_DIM)
    v = qkv[..., (N_HEADS + N_KV_HEADS) * HEAD_DIM:].reshape(b, l, N_KV_HEADS, HEAD_DIM)
    return rms_norm(q, q_gain), rms_norm(k, k_gain), v


def rope_axis(xh, pos):
    d = xh.shape[-1]
    inv = ROPE_THETA ** (-jnp.arange(0, d, 2, dtype=jnp.float32) / d)
    ang = pos.astype(jnp.float32)[:, None] * inv[None, :]
    cos = jnp.cos(ang)[None, :, None, :]
    sin = jnp.sin(ang)[None, :, None, :]
    x1, x2 = jnp.split(xh.astype(jnp.float32), 2, axis=-1)
    return jnp.concatenate([x1 * cos - x2 * sin, x2 * cos + x1 * sin], axis=-1)


def rope_2d(x, row, col):
    half = HEAD_DIM // 2
    y = jnp.concatenate([rope_axis(x[..., :half], row), rope_axis(x[..., half:], col)], axis=-1)
    return y.astype(x.dtype)


def block_attention(q, k, v):
    b, lq, _, _ = q.shape
    g = N_HEADS // N_KV_HEADS
    nb = lq // Q_BLOCK
    qb = q.reshape(b, nb, Q_BLOCK, N_KV_HEADS, g, HEAD_DIM).transpose(1, 0, 2, 3, 4, 5)
    kf = k.astype(jnp.float32)
    vf = v.astype(jnp.float32)
    scale = HEAD_DIM ** -0.5

    def one_block(qblk):
        s = jnp.einsum('bqkgd,bskd->bkgqs', qblk.astype(jnp.float32), kf) * scale
        p = jax.nn.softmax(s, axis=-1)
        return jnp.einsum('bkgqs,bskd->bqkgd', p, vf).astype(q.dtype)

    o = lax.map(one_block, qb)
    return o.transpose(1, 0, 2, 3, 4, 5).reshape(b, lq, N_HEADS * HEAD_DIM)


def modulate_norm(h, g, shift, scale):
    return rms_norm(h, g) * (1.0 + scale) + shift


def gated_residual(h, y, g, gate):
    return h + gate * rms_norm(y, g)


def setup_inputs(seed: int = 0) -> dict:
    key = jax.random.key(seed)
    ks = jax.random.split(key, 24)

    def nrm(k, shape, scale):
        return jax.random.normal(k, shape, jnp.float32) * scale

    def gain(k, shape):
        return 1.0 + 0.05 * jax.random.normal(k, shape, jnp.float32)

    kv_shape_dec = (DEC_BATCH, N_ATTN_LAYERS, PAST_LEN, N_KV_HEADS, HEAD_DIM)
    return {
        'x_prompt': nrm(ks[0], (BATCH, SEQ, D_MODEL), 1.0),
        'x_sample': nrm(ks[1], (DEC_BATCH, DEC_SEQ, D_MODEL), 1.0),
        'cache_k': nrm(ks[2], kv_shape_dec, 1.0),
        'cache_v': nrm(ks[3], kv_shape_dec, 1.0),
        'c': nrm(ks[4], (DEC_BATCH, D_MODEL), 1.0),
        'c_ctx': nrm(ks[5], (D_MODEL,), 1.0),
        'mod_w': nrm(ks[6], (DEPTH, D_MODEL, 6 * D_MODEL), 0.5 * D_MODEL ** -0.5),
        'mod_b': nrm(ks[7], (DEPTH, 6 * D_MODEL), 0.02),
        'norm_mix_pre': gain(ks[8], (DEPTH, D_MODEL)),
        'norm_mix_post': gain(ks[9], (DEPTH, D_MODEL)),
        'norm_ffn_pre': gain(ks[10], (DEPTH, D_MODEL)),
        'norm_ffn_post': gain(ks[11], (DEPTH, D_MODEL)),
        'conv_w_in': nrm(ks[12], (N_CONV_LAYERS, D_MODEL, 3 * D_MODEL), D_MODEL ** -0.5),
        'conv_k': nrm(ks[13], (N_CONV_LAYERS, CONV_WIDTH, D_MODEL), 0.5),
        'conv_w_out': nrm(ks[14], (N_CONV_LAYERS, D_MODEL, D_MODEL), D_MODEL ** -0.5),
        'attn_w_qkv': nrm(ks[15], (N_ATTN_LAYERS, D_MODEL, QKV_DIM), D_MODEL ** -0.5),
        'attn_q_gain': gain(ks[16], (N_ATTN_LAYERS, HEAD_DIM)),
        'attn_k_gain': gain(ks[17], (N_ATTN_LAYERS, HEAD_DIM)),
        'attn_w_o': nrm(ks[18], (N_ATTN_LAYERS, N_HEADS * HEAD_DIM, D_MODEL), (N_HEADS * HEAD_DIM) ** -0.5),
        'ffn_w_up': nrm(ks[19], (DEPTH, D_MODEL, 2 * D_FF), D_MODEL ** -0.5),
        'ffn_conv': nrm(ks[20], (DEPTH, CONV_WIDTH, 2 * D_FF), 0.5),
        'ffn_w_down': nrm(ks[21], (DEPTH, D_FF, D_MODEL), D_FF ** -0.5),
    }


def reference(x_prompt, x_sample, cache_k, cache_v, c, c_ctx, mod_w, mod_b,
              norm_mix_pre, norm_mix_post, norm_ffn_pre, norm_ffn_post,
              conv_w_in, conv_k, conv_w_out, attn_w_qkv, attn_q_gain, attn_k_gain, attn_w_o,
              ffn_w_up, ffn_conv, ffn_w_down):
    n_lat = x_sample.shape[1]
    rows_n = n_lat // GRID_W
    row = jnp.repeat(jnp.arange(rows_n, dtype=jnp.int32), GRID_W)
    col = jnp.tile(jnp.arange(GRID_W, dtype=jnp.int32), rows_n)

    silu_c = jax.nn.silu(c)
    silu_ctx = jax.nn.silu(c_ctx)

    h_ctx = x_prompt
    h_lat = x_sample
    new_k_list = []
    new_v_list = []
    for i in range(DEPTH):
        j = i // N_MIXERS
        mod_ctx = (silu_ctx @ mod_w[i] + mod_b[i])[None, None, :]
        mod_lat = (silu_c @ mod_w[i] + mod_b[i])[:, None, :]
        s1c, sc1c, g1c, s2c, sc2c, g2c = jnp.split(mod_ctx, 6, axis=-1)
        s1l, sc1l, g1l, s2l, sc2l, g2l = jnp.split(mod_lat, 6, axis=-1)

        u_ctx = modulate_norm(h_ctx, norm_mix_pre[i], s1c, sc1c)
        u_lat = modulate_norm(h_lat, norm_mix_pre[i], s1l, sc1l)
        if i % N_MIXERS == 0:
            y_ctx = short_conv_mixer(u_ctx, conv_w_in[j], conv_k[j], conv_w_out[j])
            y_lat = short_conv_mixer(u_lat, conv_w_in[j], conv_k[j], conv_w_out[j])
        else:
            q_c, k_c, v_c = qkv_heads(u_ctx, attn_w_qkv[j], attn_q_gain[j], attn_k_gain[j])
            y_ctx = block_attention(q_c, k_c, v_c) @ attn_w_o[j]
            new_k_list.append(k_c)
            new_v_list.append(v_c)
            q_l, k_l, v_l = qkv_heads(u_lat, attn_w_qkv[j], attn_q_gain[j], attn_k_gain[j])
            q_l = rope_2d(q_l, row, col)
            k_l = rope_2d(k_l, row, col)
            k_all = jnp.concatenate([cache_k[:, j].astype(k_l.dtype), k_l], axis=1)
            v_all = jnp.concatenate([cache_v[:, j].astype(v_l.dtype), v_l], axis=1)
            y_lat = block_attention(q_l, k_all, v_all) @ attn_w_o[j]
        h_ctx = gated_residual(h_ctx, y_ctx, norm_mix_post[i], g1c)
        h_lat = gated_residual(h_lat, y_lat, norm_mix_post[i], g1l)

        u_ctx = modulate_norm(h_ctx, norm_ffn_pre[i], s2c, sc2c)
        u_lat = modulate_norm(h_lat, norm_ffn_pre[i], s2l, sc2l)
        f_ctx = conv_ffn(u_ctx, ffn_w_up[i], ffn_conv[i], ffn_w_down[i])
        f_lat = conv_ffn(u_lat, ffn_w_up[i], ffn_conv[i], ffn_w_down[i])
        h_ctx = gated_residual(h_ctx, f_ctx, norm_ffn_post[i], g2c)
        h_lat = gated_residual(h_lat, f_lat, norm_ffn_post[i], g2l)

    new_k = jnp.stack(new_k_list, axis=1)
    new_v = jnp.stack(new_v_list, axis=1)
    return (h_ctx, h_lat, new_k, new_v)
```

```python
import numpy as np
from contextlib import ExitStack
import concourse.bass as bass
import concourse.mybir as mybir
from concourse.bass_utils import run_bass_kernel_spmd

F32 = mybir.dt.float32
BF16 = mybir.dt.bfloat16
ALU = mybir.AluOpType
AF = mybir.ActivationFunctionType

D = 1024
KC = 8
DFF = 2816
NCF = 22
TS = 2048
TP = 1024
TT = TS + TP
PAST = 512
NKEY = PAST + TS
EPS = 1e-6
UW = 1032
N_CORES = 8

P_GAIN = 0
P_MODB = 64
P_CK = P_MODB + 96
P_FCK = P_CK + 24
P_QG = P_FCK + 264
P_KG = P_QG + 1
P_CVEC = P_KG + 1
P_EPS = P_CVEC + 16
P_N = P_EPS + 1


class Prog:
    def __init__(self, nc, es):
        self.nc = nc
        self.engs = {"pe": nc.tensor, "act": nc.scalar, "dve": nc.vector, "pool": nc.gpsimd, "sp": nc.sync}
        self.sems = {}
        self.cnt = {}
        for e in self.engs:
            self.sems[e] = es.enter_context(nc.semaphore("s_" + e))
            self.cnt[e] = 0
        self.dsem = {}
        self.dcnt = {}
        self.dnext = {"pool": 0, "sp": 0}
        self.ND = 12
        for q in ("pool", "sp"):
            for i in range(self.ND):
                k = "d_%s%d" % (q, i)
                self.sems[k] = es.enter_context(nc.semaphore(k))
                self.cnt[k] = 0
        self.waited = {}
        self.lastw = {}
        self.readers = {}
        self.out_events = []

    def _need(self, reads, writes):
        need = {}

        def add(ev):
            if ev is None:
                return
            k, v = ev
            if need.get(k, 0) < v:
                need[k] = v
        for r in reads:
            add(self.lastw.get(r))
            if isinstance(r, tuple) and r[0] == "ps":
                for ev in self.readers.get(r, ()):
                    add(ev)
        for w in writes:
            add(self.lastw.get(w))
            for ev in self.readers.get(w, ()):
                add(ev)
        return need

    def _emit_waits(self, e, need):
        eng = self.engs[e]
        for k, v in need.items():
            if e == "pe" and k == "pe":
                continue
            if self.waited.get((e, k), 0) < v:
                eng.wait_ge(self.sems[k], v)
                self.waited[(e, k)] = v

    def _record(self, ev, reads, writes):
        for r in reads:
            self.readers.setdefault(r, []).append(ev)
        for w in writes:
            self.lastw[w] = ev
            self.readers[w] = []

    def op(self, e, fn, reads=(), writes=()):
        need = self._need(reads, writes)
        self._emit_waits(e, need)
        ins = fn(self.engs[e])
        self.cnt[e] += 1
        ins.then_inc(self.sems[e], 1)
        ev = (e, self.cnt[e])
        self.waited[(e, e)] = max(self.waited.get((e, e), 0), 0)
        self._record(ev, reads, writes)
        return ev

    def dma(self, q, out, in_, reads=(), writes=(), is_output=False):
        i = self.dnext[q]
        self.dnext[q] = (i + 1) % self.ND
        k = "d_%s%d" % (q, i)
        need = self._need(reads, writes)
        if self.cnt[k] > 0:
            need[k] = max(need.get(k, 0), self.cnt[k])
        self._emit_waits(q, need)
        self.engs[q].dma_start(out=out, in_=in_).then_inc(self.sems[k], 16)
        self.cnt[k] += 16
        ev = (k, self.cnt[k])
        self._record(ev, reads, writes)
        if is_output:
            self.out_events.append(ev)
        return ev

    def barrier(self, engs=("pe", "act", "dve")):
        for e in engs:
            need = {p: self.cnt[p] for p in engs if self.cnt[p] > 0}
            self._emit_waits(e, need)

    def finish(self):
        need = {}
        for k, v in self.out_events:
            need[k] = max(need.get(k, 0), v)
        self._emit_waits("sp", need)


def run(gen):
    for _ in gen:
        pass


def weave(main, side, per):
    main = iter(main)
    side = iter(side)
    credit = 0.0
    m_alive = True
    s_alive = True
    while m_alive:
        try:
            next(main)
        except StopIteration:
            m_alive = False
            break
        credit += 1.0
        while s_alive and credit >= per:
            credit -= per
            try:
                next(side)
            except StopIteration:
                s_alive = False
    if s_alive:
        for _ in side:
            pass


def build_program():
    nc = bass.Bass("TRN2", target_bir_lowering=False)

    def din(name, shape):
        return nc.dram_tensor(name, list(shape), F32, kind="ExternalInput").ap()

    def dout(name, shape):
        return nc.dram_tensor(name, list(shape), F32, kind="ExternalOutput").ap()

    x_in = din("x_in", [128, KC, TT])
    params = din("params", [128, P_N])
    modw = din("modw", [2, 24, 128, KC * 256])
    win = din("win", [8, 128, KC * 384])
    wout = din("wout", [8, 128, KC * 128])
    wqk = din("wqk", [10, 128, KC * 128])
    wv = din("wv", [128, KC * 256])
    wo = din("wo", [8, 128, KC * 128])
    wup = din("wup", [2, NCF, 128, KC * 256])
    wdn = din("wdn", [2, 8, 128, NCF * 128])
    ckT = din("ckT", [128, 2, PAST])
    cvl = din("cvl", [128, 4, 256])
    rope = din("rope", [2, 128, TS])
    rrot = din("rrot", [128, 128])
    h_out = dout("h_out", [128, KC, TT])
    k_out = dout("k_out", [128, 2, TP])
    v_out = dout("v_out", [TP, 256])

    with ExitStack() as es:
        P = Prog(nc, es)

        def sb(name, cols, dt):
            return es.enter_context(nc.sbuf_tensor(name, [128, cols], dt))

        h_t = sb("h", KC * TT, F32)
        ru_t = sb("ru", KC * UW, BF16)
        ry_t = sb("ry", KC * 516, F32)
        ra_t = sb("ra", NCF * 1024, BF16)
        wsl_t = [sb("wsl%d" % i, 3072, BF16) for i in range(3)]
        SQ = [sb("sq%d" % i, 512, BF16) for i in range(4)]
        FB = [sb("fb%d" % i, 512, F32) for i in range(4)]
        par_t = sb("par", P_N, F32)
        ones_t = sb("ones", 128, BF16)
        rrot_t = sb("rrotsb", 128, F32)
        silu_t = sb("silu", 16, BF16)
        modT_t = sb("modT", 2 * 96, F32)
        coef_t = sb("coef", 2 * 64, F32)
        hsave_t = sb("hsave", KC, F32)
        dummy_t = sb("dmy_scr", 2, F32)
        ps_t = es.enter_context(nc.psum_tensor("ps", [128, 4096], F32))

        h = h_t[:].rearrange("p (k t) -> p k t", k=KC)
        u = ru_t[:].rearrange("p (k t) -> p k t", k=KC)
        ybuf = ry_t[:].rearrange("p (k t) -> p k t", k=KC)
        evq = [ry_t[:, 516 * i:516 * i + 512] for i in range(8)]
        EVQK = [("RY", i) for i in range(8)]
        ev = [ry_t[:, i * 1032:(i + 1) * 1032] for i in range(4)]
        EVK = [[("RY", 2 * i), ("RY", 2 * i + 1)] for i in range(4)]
        pt = [ry_t[:, 516 * (4 + i):516 * (4 + i) + 256].bitcast(BF16) for i in range(4)]
        PTK = [("RY", 4 + i) for i in range(4)]
        a3 = ra_t[:].rearrange("p (c t) -> p c t", c=NCF)
        ps = ps_t[:]
        par = par_t[:]
        ones = ones_t[:]

        def bank(b, n=512, off=0):
            return ps[:, b * 512 + off: b * 512 + off + n]

        def switch_view():
            pass

        P.dma("sp", par, params, writes=["par"])
        def load_x(ui, after=()):
            for kc in range(KC):
                P.dma("sp", h[:, kc, ui * 1024:(ui + 1) * 1024], x_in[:, kc, ui * 1024:(ui + 1) * 1024],
                      reads=list(after), writes=[("h", kc, 2 * ui), ("h", kc, 2 * ui + 1)])
        load_x(0)
        P.dma("sp", rrot_t[:], rrot, writes=["rrot"])
        P.op("dve", lambda e: e.memset(ones, 1.0), writes=["ones"])
        P.op("act", lambda e: e.activation(out=silu_t[:], in_=par[:, P_CVEC:P_CVEC + 16], func=AF.Silu),
             reads=["par"], writes=["silu"])
        eps_ap = par[:, P_EPS:P_EPS + 1]

        wstate = {"i": 0}

        def load_w(src, ncols):
            i = wstate["i"]
            wstate["i"] = (i + 1) % 3
            P.dma("pool", wsl_t[i][:, 0:ncols], src, writes=[("w", i)])
            return i

        def wv3(i, inner, nk=KC):
            return wsl_t[i][:, 0:nk * inner].rearrange("p (k n) -> p k n", n=inner)

        mvs = [modT_t[:, l * 96:(l + 1) * 96].rearrange("p (c v) -> p c v", v=2) for l in range(2)]
        cfs = [coef_t[:, l * 64:(l + 1) * 64].rearrange("p (a k v) -> p a k v", a=4, k=KC) for l in range(2)]

        def mod_gen(l, parts):
            mv, cf = mvs[l], cfs[l]
            g = lambda which: par[:, P_GAIN + l * 32 + which * 8: P_GAIN + l * 32 + which * 8 + 8]
            for part in parts:
                for bi in range(4):
                    blk = part * 4 + bi
                    si = load_w(modw[l, blk], KC * 256)
                    w3 = wv3(si, 256)
                    for j in range(2):
                        ch = blk * 2 + j

                        def f(e, w3=w3, j=j, ch=ch):
                            for kc in range(KC):
                                ins = e.matmul(bank(6, 2, ch * 2), w3[:, kc, j * 128:(j + 1) * 128],
                                               silu_t[:, kc * 2:kc * 2 + 2], start=(kc == 0), stop=(kc == KC - 1))
                            return ins
                        P.op("pe", f, reads=[("w", si), "silu"], writes=[("ps", 6)])
                    yield
                modb = par[:, P_MODB + l * 48 + part * 8:P_MODB + l * 48 + part * 8 + 8]
                psv = bank(6, 16, part * 16).rearrange("p (c v) -> p c v", v=2)
                for v in range(2):
                    P.op("dve", lambda e, v=v: e.tensor_tensor(out=mv[:, part * 8:(part + 1) * 8, v], in0=psv[:, :, v], in1=modb, op=ALU.add),
                         reads=[("ps", 6), "par"], writes=[("modT", l, part)])
                spec = {1: (0, 0, True), 2: (1, 1, False), 4: (2, 2, True), 5: (3, 3, False)}.get(part)
                if spec is not None:
                    ai, which, isA = spec
                    for v in range(2):
                        src = mv[:, part * 8:(part + 1) * 8, v]
                        if isA:
                            P.op("dve", lambda e, v=v, src=src: e.scalar_tensor_tensor(
                                out=cf[:, ai, :, v], in0=src, scalar=1.0, in1=g(which), op0=ALU.add, op1=ALU.mult),
                                reads=[("modT", l, part), "par"], writes=[("coef", l, ai)])
                        else:
                            P.op("dve", lambda e, v=v, src=src: e.tensor_tensor(
                                out=cf[:, ai, :, v], in0=src, in1=g(which), op=ALU.mult),
                                reads=[("modT", l, part), "par"], writes=[("coef", l, ai)])
                yield

        class Unit:
            pass
        units = []
        for ui, name in enumerate(("S0", "S1", "P")):
            U = Unit()
            U.name = name
            U.h0 = ui * 1024
            U.v = 0 if ui < 2 else 1
            U.sample = ui < 2
            if name == "S0":
                U.ntiles = [(0, 512, 1), (512, 512, 513), (1024, 1, 1025)]
                U.zero = [(0, 1)]
                U.nh = 2
            elif name == "S1":
                U.ntiles = [(1024, 512, 1), (1536, 512, 513), (1023, 1, 0)]
                U.zero = [(1025, 1)]
                U.nh = 2
            else:
                U.ntiles = [(2048 + 256 * i, 256, 1 + 258 * i) for i in range(4)]
                U.zero = [(0, 1), (257, 2), (515, 2), (773, 2), (1031, 1)]
                U.nh = 8
            if U.sample:
                U.vt = [[(1, 512, 0)], [(513, 512, 0)]]
            else:
                U.vt = [[(1, 256, 0), (259, 256, 256)], [(517, 256, 0), (775, 256, 256)]]
            units.append(U)
        S0, S1, PU = units

        def vcols(flat, U, shift):
            if U.sample:
                return flat[:, 1 + shift:1025 + shift]
            return flat[:, 0:1032].rearrange("p (i w) -> p i w", w=258)[:, :, 1 + shift:257 + shift]

        def compact(flat, U):
            if U.sample:
                return flat[:, 0:1024]
            return flat[:, 0:1024].rearrange("p (i w) -> p i w", w=256)

        ALLRU = [("RU", kc) for kc in range(KC)]

        def make_rstd(ss_bank, n, dim, dst, dkey):
            P.op("act", lambda e: e.activation(out=dst[:, 0:n], in_=bank(ss_bank, n), func=AF.Ln, bias=eps_ap, scale=1.0 / dim),
                 reads=[("ps", ss_bank), "par"], writes=[dkey])
            P.op("act", lambda e: e.activation(out=dst[:, 0:n], in_=dst[:, 0:n], func=AF.Exp, scale=-0.5),
                 reads=[dkey], writes=[dkey])

        def save_halo():
            P.op("dve", lambda e: e.tensor_copy(out=hsave_t[:, :], in_=h[:, :, 1023]),
                 reads=[("h", kc, 1) for kc in range(KC)], writes=["hsave"])

        def norm_gen(U, l, ai, bpart, halo=True, stats_only_first=False, skip_first_stats=False, act_apply=False):
            mv, cf = mvs[l], cfs[l]
            if not stats_only_first:
                for (c0, w) in U.zero:
                    P.op("dve", lambda e, c0=c0, w=w: e.memset(u[:, :, c0:c0 + w], 0.0), writes=ALLRU)
            for ti, (h0, n, u0) in enumerate(U.ntiles):
                if n == 1 and not halo:
                    continue
                if stats_only_first and ti > 0:
                    return
                hk = [("h", kc, h0 // 512) for kc in range(KC)]
                if U.name == "S1" and n == 1:
                    hk = ["hsave"] * KC
                    hsrc = lambda kc: hsave_t[:, kc:kc + 1]
                else:
                    hsrc = lambda kc, h0=h0, n=n: h[:, kc, h0:h0 + n]
                def sqop(kc):
                    s = SQ[kc % 2]
                    P.op("act", lambda e: e.activation(out=s[:, 0:n], in_=hsrc(kc), func=AF.Square),
                         reads=[hk[kc]], writes=[("sq", kc % 2)])

                def mmop(kc):
                    s = SQ[kc % 2]
                    P.op("pe", lambda e: e.matmul(bank(7, n), ones, s[:, 0:n], start=(kc == 0), stop=(kc == KC - 1)),
                         reads=[("sq", kc % 2), "ones"], writes=[("ps", 7)])
                if not (skip_first_stats and ti == 0):
                    sqop(0)
                    sqop(1)
                    yield
                    for kc in range(KC):
                        mmop(kc)
                        if kc + 2 < KC:
                            sqop(kc + 2)
                        yield
                    make_rstd(7, n, D, FB[0], ("fb", 0))
                    yield
                if stats_only_first:
                    return
                for kc in range(KC):
                    t = FB[1 + kc % 2]
                    tk = ("fb", 1 + kc % 2)
                    P.op("dve", lambda e, kc=kc, t=t: e.scalar_tensor_tensor(
                        out=t[:, 0:n], in0=hsrc(kc), scalar=cf[:, ai, kc, U.v:U.v + 1], in1=FB[0][:, 0:n],
                        op0=ALU.mult, op1=ALU.mult), reads=[hk[kc], ("fb", 0), ("coef", l, ai)], writes=[tk])
                    if act_apply:
                        P.op("act", lambda e, kc=kc, t=t: e.activation(
                            out=u[:, kc, u0:u0 + n], in_=t[:, 0:n], func=AF.Identity,
                            bias=mv[:, bpart * 8 + kc, U.v:U.v + 1], scale=1.0),
                            reads=[tk, ("modT", l, bpart)], writes=[("RU", kc)])
                    else:
                        P.op("dve", lambda e, kc=kc, t=t: e.tensor_scalar_add(
                            out=u[:, kc, u0:u0 + n], in0=t[:, 0:n], scalar1=mv[:, bpart * 8 + kc, U.v:U.v + 1]),
                            reads=[tk, ("modT", l, bpart)], writes=[("RU", kc)])
                    yield

        ybank = {"i": 0}

        def nextbank(nb=6):
            b = ybank["i"] % nb
            ybank["i"] = b + 1
            return b

        deferred_tail = []

        def flush_tail(upto=3):
            while deferred_tail and deferred_tail[0][0] <= upto:
                deferred_tail.pop(0)[1]()

        def proj_post_gen(U, l, gi, nk, wsrc, rhs_fn, rkeys):
            cf = cfs[l]
            steps = [(ct, m) for ct in range(2) for m in range(KC)]
            banks = {}
            ybank["i"] = 0

            def A(ct, m):
                si = load_w(wsrc(m), nk * 128)
                w3 = wv3(si, 128, nk)
                b = nextbank()
                banks[(ct, m)] = b

                nsplit = nk - 3 if (ct == 0 and m == 0 and nk > 8) else nk

                def f(e, k0=0, k1=nsplit):
                    for k in range(k0, k1):
                        ins = e.matmul(bank(b), w3[:, k, :], rhs_fn(k, ct), start=(k == 0), stop=(k == nk - 1))
                    return ins
                flat = lambda lo, hi: [key for kk in range(lo, hi) for key in rkeys[kk]]
                P.op("pe", f, reads=[("w", si)] + flat(0, nsplit), writes=[("ps", b)])
                if nsplit < nk:
                    P.op("pe", lambda e: f(e, nsplit, nk), reads=[("w", si)] + flat(nsplit, nk), writes=[("ps", b)])

            def ssmm(mm):
                P.op("pe", lambda e: e.matmul(bank(6), ones, SQ[2 + mm % 2][:, :], start=(mm == 0), stop=(mm == KC - 1)),
                     reads=[("sq", 2 + mm % 2), "ones"], writes=[("ps", 6)])

            def B(ct, m):
                b = banks[(ct, m)]
                P.op("act", lambda e: e.activation(out=ybuf[:, m, 0:512], in_=bank(b), func=AF.Copy),
                     reads=[("ps", b)], writes=[("RY", m)])
                s_ = SQ[2 + m % 2]
                P.op("act", lambda e: e.activation(out=s_[:, :], in_=bank(b), func=AF.Square),
                     reads=[("ps", b)], writes=[("sq", 2 + m % 2)])

            def tail(ct):
                hc0 = U.h0 + ct * 512
                make_rstd(6, 512, D, FB[3], ("fb", 3))

                def pair(p):
                    for m in (2 * p, 2 * p + 1):
                        hk = ("h", m, hc0 // 512)
                        P.op("dve", lambda e, m=m: e.scalar_tensor_tensor(
                            out=ybuf[:, m, 0:512], in0=ybuf[:, m, 0:512], scalar=cf[:, gi, m, U.v:U.v + 1], in1=FB[3][:, :],
                            op0=ALU.mult, op1=ALU.mult), reads=[("RY", m), ("fb", 3), ("coef", l, gi)], writes=[("RY", m)])
                        P.op("dve", lambda e, m=m: e.tensor_tensor(
                            out=h[:, m, hc0:hc0 + 512], in0=h[:, m, hc0:hc0 + 512], in1=ybuf[:, m, 0:512], op=ALU.add),
                            reads=[("RY", m), hk], writes=[hk])
                if ct == 0:
                    for p in range(4):
                        pair(p)
                else:
                    for p in range(4):
                        deferred_tail.append((p, lambda p=p: pair(p)))

            A(*steps[0])
            for i, (ct, m) in enumerate(steps):
                if i + 1 < len(steps):
                    A(*steps[i + 1])
                B(ct, m)
                if m > 0:
                    ssmm(m - 1)
                yield
                if m == KC - 1:
                    ssmm(m)
                    tail(ct)

        def padded_mm(T0, U, lhs_fn, si, split=False):
            def f(e, tiles=((0, 512), (512, 512), (1024, U.nh))):
                ins = None
                for (c0, n) in tiles:
                    for kc in range(KC):
                        ins = e.matmul(ps[:, T0 * 512 + c0:T0 * 512 + c0 + n], lhs_fn(kc), u[:, kc, c0:c0 + n],
                                       start=(kc == 0), stop=(kc == KC - 1))
                return ins
            if not split:
                P.op("pe", f, reads=[("w", si)] + ALLRU, writes=[("ps", T0), ("ps", T0 + 1), ("ps", T0 + 2)])
            else:
                P.op("pe", lambda e: f(e, ((0, 512),)), reads=[("w", si)] + ALLRU, writes=[("ps", T0)])
                P.op("pe", lambda e: f(e, ((512, 512), (1024, U.nh))), reads=[("w", si)] + ALLRU, writes=[("ps", T0 + 1), ("ps", T0 + 2)])

        def tkeys(T0):
            return [("ps", T0), ("ps", T0 + 1), ("ps", T0 + 2)]

        def tflat(T0):
            return ps[:, T0 * 512:T0 * 512 + 1536]

        def conv_in_gen(U):
            if U is S0:
                save_halo()
            switch_view()
            rot = {"i": 0}

            def nextT():
                t = rot["i"]
                rot["i"] = 1 - t
                return 3 * t
            ck = lambda j, i: par[:, P_CK + j * 8 + i:P_CK + j * 8 + i + 1]
            csb = ev[1]
            r = ev[0]
            W = 1024 + U.nh
            for i in range(8):
                si = load_w(win[i], KC * 384)
                w3 = wv3(si, 384)
                Tc = nextT()
                padded_mm(Tc, U, lambda kc, w3=w3: w3[:, kc, 128:256], si)
                flush_tail()
                P.op("act", lambda e, Tc=Tc: e.activation(out=csb[:, 0:W], in_=tflat(Tc)[:, 0:W], func=AF.Copy),
                     reads=tkeys(Tc), writes=EVK[1])
                Tx = nextT()
                padded_mm(Tx, U, lambda kc, w3=w3: w3[:, kc, 256:384], si)
                P.op("dve", lambda e, Tx=Tx: e.tensor_tensor(out=csb[:, 0:W], in0=csb[:, 0:W], in1=tflat(Tx)[:, 0:W], op=ALU.mult),
                     reads=tkeys(Tx) + EVK[1], writes=EVK[1])
                Tb = nextT()
                padded_mm(Tb, U, lambda kc, w3=w3: w3[:, kc, 0:128], si)
                rc = compact(r, U)
                P.op("act", lambda e, i=i: e.activation(out=rc, in_=vcols(csb, U, 0), func=AF.Copy, scale=ck(1, i)),
                     reads=EVK[1] + ["par"], writes=EVK[0])
                P.op("dve", lambda e, i=i: e.scalar_tensor_tensor(out=rc, in0=vcols(csb, U, -1), scalar=ck(0, i), in1=rc,
                                                                  op0=ALU.mult, op1=ALU.add), reads=EVK[1] + EVK[0] + ["par"], writes=EVK[0])
                P.op("dve", lambda e, i=i: e.scalar_tensor_tensor(out=rc, in0=vcols(csb, U, 1), scalar=ck(2, i), in1=rc,
                                                                  op0=ALU.mult, op1=ALU.add), reads=EVK[1] + EVK[0] + ["par"], writes=EVK[0])
                P.op("dve", lambda e, i=i, Tb=Tb: e.tensor_tensor(out=compact(a3[:, i, :], U), in0=rc, in1=vcols(tflat(Tb), U, 0), op=ALU.mult),
                     reads=EVK[0] + tkeys(Tb), writes=[("A", i)])
                if U is S0 and i == 3:
                    load_x(2, after=[("A", 3)])
                yield

        def conv_out_gen(U):
            return proj_post_gen(U, 0, 1, KC, lambda m: wout[m],
                                 lambda k, ct: a3[:, k, ct * 512:(ct + 1) * 512], [[("A", k)] for k in range(KC)])

        def ffn_in_gen(U, l):
            if U is S0:
                save_halo()
            switch_view()
            fck = lambda j, s, c: par[:, P_FCK + l * 132 + j * 44 + s * 22 + c: P_FCK + l * 132 + j * 44 + s * 22 + c + 1]
            for c in range(NCF):
                si = load_w(wup[l, c], KC * 256)
                w3 = wv3(si, 256)
                eg = 2 * (c % 2)
                for s in range(2):
                    T0 = 3 * s
                    padded_mm(T0, U, lambda kc, w3=w3, s=s: w3[:, kc, s * 128:(s + 1) * 128], si, split=U.sample)
                    flush_tail(eg + s)
                    evk = EVK[eg + s]
                    rc = compact(ev[eg + s], U)
                    zf = tflat(T0)
                    if U.sample:
                        evb = ev[eg + s]
                        P.op("act", lambda e, evb=evb, zf=zf, s=s, c=c: e.activation(out=evb[:, 0:511], in_=zf[:, 1:512], func=AF.Copy, scale=fck(1, s, c)),
                             reads=[("ps", T0), "par"], writes=evk)
                        P.op("act", lambda e, evb=evb, zf=zf, s=s, c=c: e.activation(out=evb[:, 511:1024], in_=zf[:, 512:1025], func=AF.Copy, scale=fck(1, s, c)),
                             reads=[("ps", T0 + 1), ("ps", T0 + 2), "par"], writes=evk)
                    else:
                        P.op("act", lambda e, rc=rc, zf=zf, s=s, c=c: e.activation(out=rc, in_=vcols(zf, U, 0), func=AF.Copy, scale=fck(1, s, c)),
                             reads=tkeys(T0) + ["par"], writes=evk)
                    if s == 1:
                        P.op("act", lambda e: e.activation(out=ev[eg][:, 0:1024], in_=ev[eg][:, 0:1024], func=AF.Silu),
                             reads=EVK[eg], writes=EVK[eg])
                    P.op("dve", lambda e, rc=rc, zf=zf, s=s, c=c: e.scalar_tensor_tensor(
                        out=rc, in0=vcols(zf, U, -1), scalar=fck(0, s, c), in1=rc, op0=ALU.mult, op1=ALU.add),
                        reads=tkeys(T0) + evk + ["par"], writes=evk)
                    P.op("dve", lambda e, rc=rc, zf=zf, s=s, c=c: e.scalar_tensor_tensor(
                        out=rc, in0=vcols(zf, U, 1), scalar=fck(2, s, c), in1=rc, op0=ALU.mult, op1=ALU.add),
                        reads=tkeys(T0) + evk + ["par"], writes=evk)
                    if s == 1:
                        P.op("dve", lambda e, c=c: e.tensor_tensor(out=a3[:, c, :], in0=ev[eg][:, 0:1024], in1=ev[eg + 1][:, 0:1024], op=ALU.mult),
                             reads=EVK[eg] + EVK[eg + 1], writes=[("A", c)])
                yield

        def ffn_out_gen(U, l):
            return proj_post_gen(U, l, 3, NCF, lambda m: wdn[l, m],
                                 lambda k, ct: a3[:, k, ct * 512:(ct + 1) * 512], [[("A", k)] for k in range(NCF)])

        qb = a3[:, 0:8, :]
        kt = ra_t[:, 8 * 1024:8 * 1024 + 2 * NKEY].rearrange("p (j s) -> p j s", j=2)
        vb = ra_t[:, 13 * 1024:13 * 1024 + 20 * 256].rearrange("p (c n) -> p c n", n=256)
        tab = ra_t[:, 18 * 1024:22 * 1024].bitcast(F32).rearrange("p (a t) -> p a t", a=2)
        KA = [("A", c) for c in range(8, 13)]
        VA = [("A", c) for c in range(13, 18)]
        TA = [("A", c) for c in range(18, 22)]
        att_scale = float(128 ** -0.5)
        qg = par[:, P_QG:P_QG + 1]
        kg = par[:, P_KG:P_KG + 1]

        def attn_setup():
            P.dma("pool", kt[:, :, 0:PAST], ckT, writes=["KT"] + KA)
            P.dma("pool", vb[:, 0:4, :], cvl, writes=["V"] + VA)

        def qk_chain(U, si, w3, ct, slot, dst, dkeys, gain, stage_out):
            b = nextbank(4)
            sqb, sqk = SQ[slot], ("sq", slot)
            X, Xk = evq[2 * slot], EVQK[2 * slot]
            Y, Yk = evq[2 * slot + 1], EVQK[2 * slot + 1]
            ssb = 4 + slot
            rtb = b

            def f(e):
                if U.sample:
                    for (u0, n, off) in U.vt[ct]:
                        for kc in range(KC):
                            ins = e.matmul(bank(b, n, off), w3[:, kc, :], u[:, kc, u0:u0 + n], start=(kc == 0), stop=(kc == KC - 1))
                else:
                    o3 = bank(b).rearrange("p (a n) -> p a n", a=2)
                    for kc in range(KC):
                        r3 = u[:, kc, 0:UW].rearrange("p (i w) -> p i w", w=258)[:, 2 * ct:2 * ct + 2, 1:257]
                        ins = e.matmul(o3, w3[:, kc, :], r3, start=(kc == 0), stop=(kc == KC - 1))
                return ins
            P.op("pe", f, reads=[("w", si)] + ALLRU, writes=[("ps", b)])
            yield
            P.op("act", lambda e: e.activation(out=sqb[:, :], in_=bank(b), func=AF.Square), reads=[("ps", b)], writes=[sqk])
            P.op("pe", lambda e: e.matmul(bank(ssb), ones, sqb[:, :], start=True, stop=True), reads=[sqk, "ones"], writes=[("ps", ssb)])
            yield
            make_rstd(ssb, 512, 128, X, Xk)
            yield
            if not U.sample:
                if stage_out is None:
                    P.op("dve", lambda e: e.scalar_tensor_tensor(out=dst, in0=bank(b), scalar=gain, in1=X[:, :], op0=ALU.mult, op1=ALU.mult),
                         reads=[("ps", b), Xk, "par"], writes=dkeys)
                else:
                    P.op("dve", lambda e: e.scalar_tensor_tensor(out=Y[:, :], in0=bank(b), scalar=gain, in1=X[:, :], op0=ALU.mult, op1=ALU.mult),
                         reads=[("ps", b), Xk, "par"], writes=[Yk])
                    P.op("act", lambda e: e.activation(out=dst, in_=Y[:, :], func=AF.Copy), reads=[Yk], writes=dkeys)
                    P.dma("sp", stage_out, Y[:, :], reads=[Yk], is_output=True)
                yield
            else:
                P.op("dve", lambda e: e.scalar_tensor_tensor(out=Y[:, :], in0=bank(b), scalar=gain, in1=X[:, :], op0=ALU.mult, op1=ALU.mult),
                     reads=[("ps", b), Xk, "par"], writes=[Yk])
                P.op("pe", lambda e: e.matmul(bank(rtb), rrot_t[:], Y[:, :], start=True, stop=True), reads=[Yk, "rrot"], writes=[("ps", rtb)])
                yield
                P.op("dve", lambda e: e.tensor_tensor(out=X[:, :], in0=Y[:, :], in1=tab[:, 0, ct * 512:(ct + 1) * 512], op=ALU.mult),
                     reads=[Yk, "tab"] + TA, writes=[Xk])
                yield
                P.op("dve", lambda e: e.tensor_tensor(out=Y[:, :], in0=bank(rtb), in1=tab[:, 1, ct * 512:(ct + 1) * 512], op=ALU.mult),
                     reads=[("ps", rtb), "tab"] + TA, writes=[Yk])
                yield
                P.op("dve", lambda e: e.tensor_tensor(out=dst, in0=Y[:, :], in1=X[:, :], op=ALU.add), reads=[Yk, Xk], writes=dkeys)
                yield

        def v_gen(U):
            si = load_w(wv, KC * 256)
            w3 = wv3(si, 256)
            for tt in range(8):
                if U.sample:
                    u0 = 1 + tt * 128
                    vch = 4 + U.h0 // 128 + tt
                else:
                    u0 = 1 + 258 * (tt // 2) + 128 * (tt % 2)
                    vch = tt
                b = nextbank(4)

                def f(e, b=b, u0=u0):
                    for kc in range(KC):
                        ins = e.matmul(bank(b, 256), u[:, kc, u0:u0 + 128], w3[:, kc, :], start=(kc == 0), stop=(kc == KC - 1))
                    return ins
                P.op("pe", f, reads=[("w", si)] + ALLRU, writes=[("ps", b)])
                P.op("act", lambda e, b=b, vch=vch: e.activation(out=vb[:, vch, :], in_=bank(b, 256), func=AF.Copy),
                     reads=[("ps", b)], writes=["V"] + VA)
                if not U.sample:
                    t, tk = evq[tt % 2 * 2 + 1], EVQK[tt % 2 * 2 + 1]
                    P.op("act", lambda e, b=b, t=t: e.activation(out=t[:, 0:256], in_=bank(b, 256), func=AF.Copy), reads=[("ps", b)], writes=[tk])
                    P.dma("sp", v_out[tt * 128:(tt + 1) * 128, :], t[:, 0:256], reads=[tk], is_output=True)
                yield

        def attn_in_gen(U, kv, q):
            flush_tail()
            if U.sample:
                for a in range(2):
                    P.dma("sp", tab[:, a, :], rope[a, :, U.h0:U.h0 + 1024], writes=["tab"] + TA)
            chains = []
            heads = ([8, 9] if kv else []) + (list(range(8)) if q else [])
            kbase = (PAST + U.h0) if U.sample else 0
            active = []
            slot_free = [0, 1, 2, 3]
            if kv:
                yield from v_gen(U)
            for hd in heads:
                si = load_w(wqk[hd], KC * 128)
                w3 = wv3(si, 128)
                for ct in range(2):
                    if hd >= 8:
                        j = hd - 8
                        dst = kt[:, j, kbase + ct * 512:kbase + (ct + 1) * 512]
                        dkeys = ["QK"] + KA
                        so = None if U.sample else k_out[:, j, ct * 512:(ct + 1) * 512]
                        gain = kg
                    else:
                        dst = qb[:, hd, ct * 512:(ct + 1) * 512]
                        dkeys = [("A", hd), ("Aq", hd, ct * 512), ("Aq", hd, ct * 512 + 256)]
                        so = None
                        gain = qg
                    while not slot_free:
                        for (g_, s_) in list(active):
                            try:
                                next(g_)
                            except StopIteration:
                                active.remove((g_, s_))
                                slot_free.append(s_)
                        yield
                    s_ = slot_free.pop(0)
                    g_ = qk_chain(U, si, w3, ct, s_, dst, dkeys, gain, so)
                    next(g_)
                    active.append((g_, s_))
            while active:
                for (g_, s_) in list(active):
                    try:
                        next(g_)
                    except StopIteration:
                        active.remove((g_, s_))
                yield

        def attend_gen(U):
            grp = {"i": 0}
            sbk = {"i": 0}
            pti = {"i": 0}
            if U.sample:
                jobs = [(hd, 1, qt * 512, 512, [(c * 128, c) for c in range(20)]) for hd in range(8) for qt in range(2)]
            else:
                jobs = [(hp * 2, 2, sq * 256, 256, [(sq * 256 + c * 128, 2 * sq + c) for c in range(2)])
                        for sq in range(4) for hp in range(4)]
            pending_fin = []
            for (hd, nh, q0, nq, keys) in jobs:
                j = hd // 4
                g = grp["i"]
                grp["i"] = 1 - g
                ob, db = 3 + g, 5 + g
                pend = []
                NQ = nh * nq
                qkeys = [("Aq", hd + a, q0 + c0) for a in range(nh) for c0 in ((0, 256) if nq == 512 else (0,))]
                q_ap = qb[:, hd:hd + nh, q0:q0 + nq]

                def v3(ap2d):
                    return ap2d.rearrange("p (a n) -> p a n", a=nh)

                def pv(item, first, last):
                    (pi, vch) = item
                    P.op("pe", lambda e: e.matmul(bank(ob, NQ), vb[:, vch, j * 128:(j + 1) * 128], pt[pi][:, 0:NQ], start=first, stop=last),
                         reads=[PTK[pi], "V"] + VA, writes=[("ps", ob)])
                    P.op("pe", lambda e: e.matmul(bank(db, NQ), ones, pt[pi][:, 0:NQ], start=first, stop=last),
                         reads=[PTK[pi], "ones"], writes=[("ps", db)])
                done = 0
                for ki, (k0, vch) in enumerate(keys):
                    sbank = sbk["i"]
                    sbk["i"] = (sbank + 1) % 3
                    pi = pti["i"]
                    pti["i"] = (pi + 1) % 4
                    P.op("pe", lambda e: e.matmul(v3(bank(sbank, NQ)), kt[:, j, k0:k0 + 128], q_ap, start=True, stop=True),
                         reads=["KT", "QK"] + qkeys + KA, writes=[("ps", sbank)])
                    P.op("act", lambda e: e.activation(out=pt[pi][:, 0:NQ], in_=bank(sbank, NQ), func=AF.Exp, scale=att_scale),
                         reads=[("ps", sbank)], writes=[PTK[pi]])
                    pend.append((pi, vch))
                    if len(pend) > 2:
                        pv(pend.pop(0), done == 0, False)
                        done += 1
                    if ki == min(1, len(keys) - 1) and pending_fin:
                        pending_fin.pop(0)()
                    yield
                while pend:
                    it = pend.pop(0)
                    pv(it, done == 0, len(pend) == 0)
                    done += 1
                def fin(ob=ob, db=db, NQ=NQ, q_ap=q_ap, qkeys=qkeys, v3=v3):
                    if U.sample:
                        P.op("dve", lambda e: e.reciprocal(out=FB[3][:, 0:NQ], in_=bank(db, NQ)), reads=[("ps", db)], writes=[("fb", 3)])
                    else:
                        P.op("act", lambda e: e.activation(out=FB[3][:, 0:NQ], in_=bank(db, NQ), func=AF.Ln), reads=[("ps", db)], writes=[("fb", 3)])
                        P.op("act", lambda e: e.activation(out=FB[3][:, 0:NQ], in_=FB[3][:, 0:NQ], func=AF.Exp, scale=-1.0), reads=[("fb", 3)], writes=[("fb", 3)])
                    P.op("dve", lambda e: e.tensor_tensor(out=q_ap, in0=v3(bank(ob, NQ)), in1=v3(FB[3][:, 0:NQ]), op=ALU.mult),
                         reads=[("ps", ob), ("fb", 3)], writes=qkeys)
                pending_fin.append(fin)
                yield
            while pending_fin:
                pending_fin.pop(0)()

        def attn_out_gen(U):
            yield from attend_gen(U)
            yield from proj_post_gen(U, 1, 1, KC, lambda m: wo[m],
                                     lambda k, ct: qb[:, k, ct * 512:(ct + 1) * 512],
                                     [[("A", k)] + [("Aq", k, c0) for c0 in (0, 256, 512, 768)] for k in range(KC)])

        class Stage:
            def __init__(self, norm, inn, out, per=0.2):
                self.norm, self.inn, self.out, self.per = norm, inn, out, per
                self.norm_done = False

        def chain(*gens):
            for g_ in gens:
                yield from g_

        stages = []
        for U in units:
            inn = (lambda U=U: conv_in_gen(U))
            if U is S0:
                inn = (lambda U=U: weave_gen(conv_in_gen(U), mod_gen(0, [2, 3]), 0.8))
            if U is S1:
                inn = (lambda U=U: weave_gen(conv_in_gen(U), mod_gen(0, [4]), 1.6))
            if U is PU:
                inn = (lambda U=U: weave_gen(conv_in_gen(U), mod_gen(0, [5]), 1.6))
            stages.append(Stage(lambda U=U: norm_gen(U, 0, 0, 0, skip_first_stats=(U is S0), act_apply=(U is not S0)), inn, lambda U=U: conv_out_gen(U)))
        for U in units:
            inn = (lambda U=U: ffn_in_gen(U, 0))
            if U is S1:
                inn = (lambda U=U: weave_gen(ffn_in_gen(U, 0), mod_gen(1, [0, 1, 2]), 1.4))
            if U is PU:
                inn = (lambda U=U: weave_gen(ffn_in_gen(U, 0), mod_gen(1, [3, 4, 5]), 1.4))
            stages.append(Stage(lambda U=U: norm_gen(U, 0, 2, 3, act_apply=(U is S0)), inn, lambda U=U: ffn_out_gen(U, 0)))
        stages.append(Stage(lambda: norm_gen(S0, 1, 0, 0, halo=False), lambda: chain_setup(attn_in_gen(S0, True, False)), None))
        stages.append(Stage(lambda: norm_gen(S1, 1, 0, 0, halo=False), lambda: attn_in_gen(S1, True, True), lambda: attn_out_gen(S1), per=6.0))
        stages.append(Stage(lambda: norm_gen(S0, 1, 0, 0, halo=False), lambda: attn_in_gen(S0, False, True), lambda: attn_out_gen(S0), per=4.4))
        stages.append(Stage(lambda: norm_gen(PU, 1, 0, 0, halo=False), lambda: chain(attn_in_gen(PU, True, True), attend_gen(PU)),
                            lambda: proj_post_gen(PU, 1, 1, KC, lambda m: wo[m], lambda k, ct: qb[:, k, ct * 512:(ct + 1) * 512],
                                                  [[("A", k)] + [("Aq", k, c0) for c0 in (0, 256, 512, 768)] for k in range(KC)])))
        for U in units:
            stages.append(Stage(lambda U=U: norm_gen(U, 1, 2, 3), lambda U=U: ffn_in_gen(U, 1), lambda U=U: ffn_out_gen(U, 1)))

        def weave_gen(main, side, per):
            side = iter(side)
            credit = 0.0
            s_alive = True
            for _ in main:
                credit += 1.0
                while s_alive and credit >= per:
                    credit -= per
                    try:
                        next(side)
                    except StopIteration:
                        s_alive = False
                yield
            if s_alive:
                for _ in side:
                    yield

        def chain_setup(g_):
            yield from g_
            attn_setup()

        run(norm_gen(S0, 0, 0, 0, stats_only_first=True))
        run(mod_gen(0, [0, 1]))
        load_x(1, after=[("coef", 0, 0)])
        for i, st in enumerate(stages):
            if not st.norm_done:
                flush_tail()
                run(st.norm())
            run(st.inn())
            nxt = stages[i + 1] if i + 1 < len(stages) else None
            if st.out is not None:
                if nxt is not None:
                    weave(st.out(), nxt.norm(), st.per)
                    nxt.norm_done = True
                else:
                    run(st.out())
        flush_tail()
        for ui in range(3):
            for kc in range(KC):
                P.dma("sp", h_out[:, kc, ui * 1024:(ui + 1) * 1024], h[:, kc, ui * 1024:(ui + 1) * 1024],
                      reads=[("h", kc, 2 * ui), ("h", kc, 2 * ui + 1)], is_output=True)
        P.finish()
    return nc


def _fm(w):
    return w.reshape(KC, 128, w.shape[1]).transpose(1, 0, 2)


def _host_layout(inp):
    f = lambda a: np.ascontiguousarray(a, dtype=np.float32)
    shared = {}
    mod_w = inp["mod_w"]
    shared["modw"] = f(np.stack([np.stack([_fm(mod_w[l][:, blk * 256:(blk + 1) * 256]).reshape(128, KC * 256)
                                           for blk in range(24)]) for l in range(2)]))
    w_in = inp["conv_w_in"][0]
    shared["win"] = f(np.stack([np.concatenate([_fm(w_in[:, s * 1024 + i * 128: s * 1024 + (i + 1) * 128]) for s in range(3)], axis=2)
                                .reshape(128, KC * 384) for i in range(8)]))
    w_o1 = inp["conv_w_out"][0]
    shared["wout"] = f(np.stack([_fm(w_o1[:, m * 128:(m + 1) * 128]).reshape(128, KC * 128) for m in range(8)]))
    wqkv = inp["attn_w_qkv"][0]
    shared["wqk"] = f(np.stack([_fm(wqkv[:, hd * 128:(hd + 1) * 128]).reshape(128, KC * 128) for hd in range(10)]))
    shared["wv"] = f(_fm(wqkv[:, 1280:1536]).reshape(128, KC * 256))
    w_o2 = inp["attn_w_o"][0]
    shared["wo"] = f(np.stack([_fm(w_o2[:, m * 128:(m + 1) * 128]).reshape(128, KC * 128) for m in range(8)]))
    wup = inp["ffn_w_up"]
    shared["wup"] = f(np.stack([np.stack([np.concatenate([_fm(wup[l][:, s * DFF + c * 128: s * DFF + (c + 1) * 128]) for s in range(2)], axis=2)
                                          .reshape(128, KC * 256) for c in range(NCF)]) for l in range(2)]))
    wdn = inp["ffn_w_down"]
    shared["wdn"] = f(np.stack([np.stack([wdn[l][:, m * 128:(m + 1) * 128].reshape(NCF, 128, 128).transpose(1, 0, 2).reshape(128, NCF * 128)
                                          for m in range(8)]) for l in range(2)]))
    t = np.arange(TS)
    row = (t // 64).astype(np.float32)
    col = (t % 64).astype(np.float32)
    inv = (10000.0 ** (-np.arange(0, 64, 2, dtype=np.float32) / 64)).astype(np.float32)
    cos = np.zeros((128, TS), np.float32)
    sin = np.zeros((128, TS), np.float32)
    for d in range(128):
        pos = row if d < 64 else col
        ang = (pos * inv[d % 32]).astype(np.float32)
        cos[d] = np.cos(ang)
        sin[d] = np.sin(ang)
    shared["rope"] = f(np.stack([cos, sin]))
    R = np.zeros((128, 128), np.float32)
    for d in range(128):
        blk = (d // 64) * 64
        dd = d % 64
        if dd < 32:
            R[blk + dd + 32, d] = -1.0
        else:
            R[blk + dd - 32, d] = 1.0
    shared["rrot"] = R
    par = np.zeros((128, P_N), np.float32)
    gains = [inp["norm_mix_pre"], inp["norm_mix_post"], inp["norm_ffn_pre"], inp["norm_ffn_post"]]
    for l in range(2):
        for wi, g in enumerate(gains):
            par[:, P_GAIN + l * 32 + wi * 8: P_GAIN + l * 32 + wi * 8 + 8] = g[l].reshape(KC, 128).T
        par[:, P_MODB + l * 48: P_MODB + (l + 1) * 48] = inp["mod_b"][l].reshape(48, 128).T
        fc = inp["ffn_conv"][l]
        for j in range(3):
            par[:, P_FCK + l * 132 + j * 44: P_FCK + l * 132 + (j + 1) * 44] = fc[j].reshape(44, 128).T
    ck = inp["conv_k"][0]
    for j in range(3):
        par[:, P_CK + j * 8: P_CK + (j + 1) * 8] = ck[j].reshape(8, 128).T
    par[:, P_QG] = inp["attn_q_gain"][0]
    par[:, P_KG] = inp["attn_k_gain"][0]
    par[:, P_EPS] = EPS
    cctx = inp["c_ctx"].reshape(KC, 128).T
    in_maps = []
    for b in range(N_CORES):
        m = dict(shared)
        p = par.copy()
        cv = np.stack([inp["c"][b].reshape(KC, 128).T, cctx], axis=2)
        p[:, P_CVEC:P_CVEC + 16] = cv.reshape(128, 16)
        m["params"] = p
        xs = inp["x_sample"][b]
        xp = inp["x_prompt"][4 * b:4 * b + 4].reshape(TP, D)
        xa = np.concatenate([xs, xp], axis=0)
        m["x_in"] = f(xa.reshape(TT, KC, 128).transpose(2, 1, 0))
        m["ckT"] = f(inp["cache_k"][b, 0].transpose(2, 1, 0))
        m["cvl"] = f(inp["cache_v"][b, 0].reshape(4, 128, 256).transpose(1, 0, 2))
        in_maps.append(m)
    return in_maps


_NC_CACHE = {}


def kernel(**inputs):
    inp = {k: np.asarray(v) for k, v in inputs.items()}
    in_maps = _host_layout(inp)
    if "nc" not in _NC_CACHE:
        _NC_CACHE["nc"] = build_program()
    nc = _NC_CACHE["nc"]
    res = run_bass_kernel_spmd(nc, in_maps, core_ids=list(range(N_CORES)))
    y_prompt = np.zeros((32, 256, D), np.float32)
    y_sample = np.zeros((8, TS, D), np.float32)
    new_k = np.zeros((32, 1, 256, 2, 128), np.float32)
    new_v = np.zeros((32, 1, 256, 2, 128), np.float32)
    for b in range(N_CORES):
        r = res.results[b]
        ho = np.asarray(r["h_out"]).transpose(2, 1, 0).reshape(TT, D)
        y_sample[b] = ho[:TS]
        y_prompt[4 * b:4 * b + 4] = ho[TS:].reshape(4, 256, D)
        ko = np.asarray(r["k_out"]).transpose(2, 1, 0)
        new_k[4 * b:4 * b + 4, 0] = ko.reshape(4, 256, 2, 128)
        vo = np.asarray(r["v_out"]).reshape(4, 256, 2, 128)
        new_v[4 * b:4 * b + 4, 0] = vo
    return (y_prompt, y_sample, new_k, new_v)
```

```python
import numpy as np
from contextlib import ExitStack
import concourse.bass as bass
import concourse.mybir as mybir
from concourse.bass_utils import run_bass_kernel_spmd

F32 = mybir.dt.float32
BF16 = mybir.dt.bfloat16
ALU = mybir.AluOpType
AF = mybir.ActivationFunctionType

D = 1024
KC = 8
DFF = 2816
NCF = 22
TS = 2048
TP = 1024
TT = TS + TP
PAST = 512
NKEY = PAST + TS
EPS = 1e-6
UW = 1032
N_CORES = 8

P_GAIN = 0
P_MODB = 64
P_CK = P_MODB + 96
P_FCK = P_CK + 24
P_QG = P_FCK + 264
P_KG = P_QG + 1
P_CVEC = P_KG + 1
P_EPS = P_CVEC + 16
P_N = P_EPS + 1


class Prog:
    def __init__(self, nc, es):
        self.nc = nc
        self.engs = {"pe": nc.tensor, "act": nc.scalar, "dve": nc.vector, "pool": nc.gpsimd, "sp": nc.sync}
        self.sems = {}
        self.cnt = {}
        for e in self.engs:
            self.sems[e] = es.enter_context(nc.semaphore("s_" + e))
            self.cnt[e] = 0
        self.dsem = {}
        self.dcnt = {}
        self.dnext = {"pool": 0, "sp": 0}
        self.ND = 12
        for q in ("pool", "sp"):
            for i in range(self.ND):
                k = "d_%s%d" % (q, i)
                self.sems[k] = es.enter_context(nc.semaphore(k))
                self.cnt[k] = 0
        self.waited = {}
        self.lastw = {}
        self.readers = {}
        self.out_events = []

    def _need(self, reads, writes):
        need = {}

        def add(ev):
            if ev is None:
                return
            k, v = ev
            if need.get(k, 0) < v:
                need[k] = v
        for r in reads:
            add(self.lastw.get(r))
            if isinstance(r, tuple) and r[0] == "ps":
                for ev in self.readers.get(r, ()):
                    add(ev)
        for w in writes:
            add(self.lastw.get(w))
            for ev in self.readers.get(w, ()):
                add(ev)
        return need

    def _emit_waits(self, e, need):
        eng = self.engs[e]
        for k, v in need.items():
            if e == "pe" and k == "pe":
                continue
            if self.waited.get((e, k), 0) < v:
                eng.wait_ge(self.sems[k], v)
                self.waited[(e, k)] = v

    def _record(self, ev, reads, writes):
        for r in reads:
            self.readers.setdefault(r, []).append(ev)
        for w in writes:
            self.lastw[w] = ev
            self.readers[w] = []

    def op(self, e, fn, reads=(), writes=()):
        need = self._need(reads, writes)
        self._emit_waits(e, need)
        ins = fn(self.engs[e])
        self.cnt[e] += 1
        ins.then_inc(self.sems[e], 1)
        ev = (e, self.cnt[e])
        self.waited[(e, e)] = max(self.waited.get((e, e), 0), 0)
        self._record(ev, reads, writes)
        return ev

    def dma(self, q, out, in_, reads=(), writes=(), is_output=False):
        i = self.dnext[q]
        self.dnext[q] = (i + 1) % self.ND
        k = "d_%s%d" % (q, i)
        need = self._need(reads, writes)
        if self.cnt[k] > 0:
            need[k] = max(need.get(k, 0), self.cnt[k])
        self._emit_waits(q, need)
        self.engs[q].dma_start(out=out, in_=in_).then_inc(self.sems[k], 16)
        self.cnt[k] += 16
        ev = (k, self.cnt[k])
        self._record(ev, reads, writes)
        if is_output:
            self.out_events.append(ev)
        return ev

    def barrier(self, engs=("pe", "act", "dve")):
        for e in engs:
            need = {p: self.cnt[p] for p in engs if self.cnt[p] > 0}
            self._emit_waits(e, need)

    def finish(self):
        need = {}
        for k, v in self.out_events:
            need[k] = max(need.get(k, 0), v)
        self._emit_waits("sp", need)


def run(gen):
    for _ in gen:
        pass


def weave(main, side, per):
    main = iter(main)
    side = iter(side)
    credit = 0.0
    m_alive = True
    s_alive = True
    while m_alive:
        try:
            next(main)
        except StopIteration:
            m_alive = False
            break
        credit += 1.0
        while s_alive and credit >= per:
            credit -= per
            try:
                next(side)
            except StopIteration:
                s_alive = False
    if s_alive:
        for _ in side:
            pass


def build_program():
    nc = bass.Bass("TRN2", target_bir_lowering=False)

    def din(name, shape):
        return nc.dram_tensor(name, list(shape), F32, kind="ExternalInput").ap()

    def dout(name, shape):
        return nc.dram_tensor(name, list(shape), F32, kind="ExternalOutput").ap()

    x_in = din("x_in", [128, KC, TT])
    params = din("params", [128, P_N])
    modw = din("modw", [2, 24, 128, KC * 256])
    win = din("win", [8, 128, KC * 384])
    wout = din("wout", [8, 128, KC * 128])
    wqk = din("wqk", [10, 128, KC * 128])
    wv = din("wv", [128, KC * 256])
    wo = din("wo", [8, 128, KC * 128])
    wup = din("wup", [2, NCF, 128, KC * 256])
    wdn = din("wdn", [2, 8, 128, NCF * 128])
    ckT = din("ckT", [128, 2, PAST])
    cvl = din("cvl", [128, 4, 256])
    rope = din("rope", [2, 128, TS])
    rrot = din("rrot", [128, 128])
    h_out = dout("h_out", [128, KC, TT])
    k_out = dout("k_out", [128, 2, TP])
    v_out = dout("v_out", [TP, 256])

    with ExitStack() as es:
        P = Prog(nc, es)

        def sb(name, cols, dt):
            return es.enter_context(nc.sbuf_tensor(name, [128, cols], dt))

        h_t = sb("h", KC * TT, F32)
        ru_t = sb("ru", KC * UW, BF16)
        ry_t = sb("ry", KC * 516, F32)
        ra_t = sb("ra", NCF * 1024, BF16)
        wsl_t = [sb("wsl%d" % i, 3072, BF16) for i in range(3)]
        SQ = [sb("sq%d" % i, 512, BF16) for i in range(4)]
        FB = [sb("fb%d" % i, 512, F32) for i in range(4)]
        par_t = sb("par", P_N, F32)
        ones_t = sb("ones", 128, BF16)
        rrot_t = sb("rrotsb", 128, F32)
        silu_t = sb("silu", 16, BF16)
        modT_t = sb("modT", 2 * 96, F32)
        coef_t = sb("coef", 2 * 64, F32)
        hsave_t = sb("hsave", KC, F32)
        dummy_t = sb("dmy_scr", 2, F32)
        ps_t = es.enter_context(nc.psum_tensor("ps", [128, 4096], F32))

        h = h_t[:].rearrange("p (k t) -> p k t", k=KC)
        u = ru_t[:].rearrange("p (k t) -> p k t", k=KC)
        ybuf = ry_t[:].rearrange("p (k t) -> p k t", k=KC)
        evq = [ry_t[:, 516 * i:516 * i + 512] for i in range(8)]
        EVQK = [("RY", i) for i in range(8)]
        ev = [ry_t[:, i * 1032:(i + 1) * 1032] for i in range(4)]
        EVK = [[("RY", 2 * i), ("RY", 2 * i + 1)] for i in range(4)]
        pt = [ry_t[:, 516 * (4 + i):516 * (4 + i) + 256].bitcast(BF16) for i in range(4)]
        PTK = [("RY", 4 + i) for i in range(4)]
        a3 = ra_t[:].rearrange("p (c t) -> p c t", c=NCF)
        ps = ps_t[:]
        par = par_t[:]
        ones = ones_t[:]

        def bank(b, n=512, off=0):
            return ps[:, b * 512 + off: b * 512 + off + n]

        def switch_view():
            pass

        P.dma("sp", par, params, writes=["par"])
        def load_x(ui, after=()):
            for kc in range(KC):
                P.dma("sp", h[:, kc, ui * 1024:(ui + 1) * 1024], x_in[:, kc, ui * 1024:(ui + 1) * 1024],
                      reads=list(after), writes=[("h", kc, 2 * ui), ("h", kc, 2 * ui + 1)])
        load_x(0)
        P.dma("sp", rrot_t[:], rrot, writes=["rrot"])
        P.op("dve", lambda e: e.memset(ones, 1.0), writes=["ones"])
        P.op("act", lambda e: e.activation(out=silu_t[:], in_=par[:, P_CVEC:P_CVEC + 16], func=AF.Silu),
             reads=["par"], writes=["silu"])
        eps_ap = par[:, P_EPS:P_EPS + 1]

        wstate = {"i": 0}

        def load_w(src, ncols):
            i = wstate["i"]
            wstate["i"] = (i + 1) % 3
            P.dma("pool", wsl_t[i][:, 0:ncols], src, writes=[("w", i)])
            return i

        def wv3(i, inner, nk=KC):
            return wsl_t[i][:, 0:nk * inner].rearrange("p (k n) -> p k n", n=inner)

        mvs = [modT_t[:, l * 96:(l + 1) * 96].rearrange("p (c v) -> p c v", v=2) for l in range(2)]
        cfs = [coef_t[:, l * 64:(l + 1) * 64].rearrange("p (a k v) -> p a k v", a=4, k=KC) for l in range(2)]

        def mod_gen(l, parts):
            mv, cf = mvs[l], cfs[l]
            g = lambda which: par[:, P_GAIN + l * 32 + which * 8: P_GAIN + l * 32 + which * 8 + 8]
            for part in parts:
                for bi in range(4):
                    blk = part * 4 + bi
                    si = load_w(modw[l, blk], KC * 256)
                    w3 = wv3(si, 256)
                    for j in range(2):
                        ch = blk * 2 + j

                        def f(e, w3=w3, j=j, ch=ch):
                            for kc in range(KC):
                                ins = e.matmul(bank(6, 2, ch * 2), w3[:, kc, j * 128:(j + 1) * 128],
                                               silu_t[:, kc * 2:kc * 2 + 2], start=(kc == 0), stop=(kc == KC - 1))
                            return ins
                        P.op("pe", f, reads=[("w", si), "silu"], writes=[("ps", 6)])
                    yield
                modb = par[:, P_MODB + l * 48 + part * 8:P_MODB + l * 48 + part * 8 + 8]
                psv = bank(6, 16, part * 16).rearrange("p (c v) -> p c v", v=2)
                for v in range(2):
                    P.op("dve", lambda e, v=v: e.tensor_tensor(out=mv[:, part * 8:(part + 1) * 8, v], in0=psv[:, :, v], in1=modb, op=ALU.add),
                         reads=[("ps", 6), "par"], writes=[("modT", l, part)])
                spec = {1: (0, 0, True), 2: (1, 1, False), 4: (2, 2, True), 5: (3, 3, False)}.get(part)
                if spec is not None:
                    ai, which, isA = spec
                    for v in range(2):
                        src = mv[:, part * 8:(part + 1) * 8, v]
                        if isA:
                            P.op("dve", lambda e, v=v, src=src: e.scalar_tensor_tensor(
                                out=cf[:, ai, :, v], in0=src, scalar=1.0, in1=g(which), op0=ALU.add, op1=ALU.mult),
                                reads=[("modT", l, part), "par"], writes=[("coef", l, ai)])
                        else:
                            P.op("dve", lambda e, v=v, src=src: e.tensor_tensor(
                                out=cf[:, ai, :, v], in0=src, in1=g(which), op=ALU.mult),
                                reads=[("modT", l, part), "par"], writes=[("coef", l, ai)])
                yield

        class Unit:
            pass
        units = []
        for ui, name in enumerate(("S0", "S1", "P")):
            U = Unit()
            U.name = name
            U.h0 = ui * 1024
            U.v = 0 if ui < 2 else 1
            U.sample = ui < 2
            if name == "S0":
                U.ntiles = [(0, 512, 1), (512, 512, 513), (1024, 1, 1025)]
                U.zero = [(0, 1)]
                U.nh = 2
            elif name == "S1":
                U.ntiles = [(1024, 512, 1), (1536, 512, 513), (1023, 1, 0)]
                U.zero = [(1025, 1)]
                U.nh = 2
            else:
                U.ntiles = [(2048 + 256 * i, 256, 1 + 258 * i) for i in range(4)]
                U.zero = [(0, 1), (257, 2), (515, 2), (773, 2), (1031, 1)]
                U.nh = 8
            if U.sample:
                U.vt = [[(1, 512, 0)], [(513, 512, 0)]]
            else:
                U.vt = [[(1, 256, 0), (259, 256, 256)], [(517, 256, 0), (775, 256, 256)]]
            units.append(U)
        S0, S1, PU = units

        def vcols(flat, U, shift):
            if U.sample:
                return flat[:, 1 + shift:1025 + shift]
            return flat[:, 0:1032].rearrange("p (i w) -> p i w", w=258)[:, :, 1 + shift:257 + shift]

        def compact(flat, U):
            if U.sample:
                return flat[:, 0:1024]
            return flat[:, 0:1024].rearrange("p (i w) -> p i w", w=256)

        ALLRU = [("RU", kc) for kc in range(KC)]

        def make_rstd(ss_bank, n, dim, dst, dkey):
            P.op("act", lambda e: e.activation(out=dst[:, 0:n], in_=bank(ss_bank, n), func=AF.Ln, bias=eps_ap, scale=1.0 / dim),
                 reads=[("ps", ss_bank), "par"], writes=[dkey])
            P.op("act", lambda e: e.activation(out=dst[:, 0:n], in_=dst[:, 0:n], func=AF.Exp, scale=-0.5),
                 reads=[dkey], writes=[dkey])

        def save_halo():
            P.op("dve", lambda e: e.tensor_copy(out=hsave_t[:, :], in_=h[:, :, 1023]),
                 reads=[("h", kc, 1) for kc in range(KC)], writes=["hsave"])

        def norm_gen(U, l, ai, bpart, halo=True, stats_only_first=False, skip_first_stats=False):
            mv, cf = mvs[l], cfs[l]
            if not stats_only_first:
                for (c0, w) in U.zero:
                    P.op("dve", lambda e, c0=c0, w=w: e.memset(u[:, :, c0:c0 + w], 0.0), writes=ALLRU)
            for ti, (h0, n, u0) in enumerate(U.ntiles):
                if n == 1 and not halo:
                    continue
                if stats_only_first and ti > 0:
                    return
                hk = [("h", kc, h0 // 512) for kc in range(KC)]
                if U.name == "S1" and n == 1:
                    hk = ["hsave"] * KC
                    hsrc = lambda kc: hsave_t[:, kc:kc + 1]
                else:
                    hsrc = lambda kc, h0=h0, n=n: h[:, kc, h0:h0 + n]
                def sqop(kc):
                    s = SQ[kc % 2]
                    P.op("act", lambda e: e.activation(out=s[:, 0:n], in_=hsrc(kc), func=AF.Square),
                         reads=[hk[kc]], writes=[("sq", kc % 2)])

                def mmop(kc):
                    s = SQ[kc % 2]
                    P.op("pe", lambda e: e.matmul(bank(7, n), ones, s[:, 0:n], start=(kc == 0), stop=(kc == KC - 1)),
                         reads=[("sq", kc % 2), "ones"], writes=[("ps", 7)])
                if not (skip_first_stats and ti == 0):
                    sqop(0)
                    sqop(1)
                    yield
                    for kc in range(KC):
                        mmop(kc)
                        if kc + 2 < KC:
                            sqop(kc + 2)
                        yield
                    make_rstd(7, n, D, FB[0], ("fb", 0))
                    yield
                if stats_only_first:
                    return
                for kc in range(KC):
                    t = FB[1 + kc % 2]
                    tk = ("fb", 1 + kc % 2)
                    P.op("dve", lambda e, kc=kc, t=t: e.scalar_tensor_tensor(
                        out=t[:, 0:n], in0=hsrc(kc), scalar=cf[:, ai, kc, U.v:U.v + 1], in1=FB[0][:, 0:n],
                        op0=ALU.mult, op1=ALU.mult), reads=[hk[kc], ("fb", 0), ("coef", l, ai)], writes=[tk])
                    P.op("dve", lambda e, kc=kc, t=t: e.tensor_scalar_add(
                        out=u[:, kc, u0:u0 + n], in0=t[:, 0:n], scalar1=mv[:, bpart * 8 + kc, U.v:U.v + 1]),
                        reads=[tk, ("modT", l, bpart)], writes=[("RU", kc)])
                    yield

        ybank = {"i": 0}

        def nextbank(nb=6):
            b = ybank["i"] % nb
            ybank["i"] = b + 1
            return b

        deferred_tail = []

        def flush_tail(upto=3):
            while deferred_tail and deferred_tail[0][0] <= upto:
                deferred_tail.pop(0)[1]()

        def proj_post_gen(U, l, gi, nk, wsrc, rhs_fn, rkeys, ystart=0):
            cf = cfs[l]
            steps = [(ct, m) for ct in range(2) for m in range(KC)]
            banks = {}
            ybank["i"] = ystart

            def A(ct, m):
                si = load_w(wsrc(m), nk * 128)
                w3 = wv3(si, 128, nk)
                b = nextbank()
                banks[(ct, m)] = b

                nsplit = nk - 3 if (ct == 0 and m == 0 and nk > 8) else nk

                def f(e, k0=0, k1=nsplit):
                    for k in range(k0, k1):
                        ins = e.matmul(bank(b), w3[:, k, :], rhs_fn(k, ct), start=(k == 0), stop=(k == nk - 1))
                    return ins
                flat = lambda lo, hi: [key for kk in range(lo, hi) for key in rkeys[kk]]
                P.op("pe", f, reads=[("w", si)] + flat(0, nsplit), writes=[("ps", b)])
                if nsplit < nk:
                    P.op("pe", lambda e: f(e, nsplit, nk), reads=[("w", si)] + flat(nsplit, nk), writes=[("ps", b)])

            def ssmm(mm):
                P.op("pe", lambda e: e.matmul(bank(6), ones, SQ[2 + mm % 2][:, :], start=(mm == 0), stop=(mm == KC - 1)),
                     reads=[("sq", 2 + mm % 2), "ones"], writes=[("ps", 6)])

            def B(ct, m):
                b = banks[(ct, m)]
                P.op("act", lambda e: e.activation(out=ybuf[:, m, 0:512], in_=bank(b), func=AF.Copy),
                     reads=[("ps", b)], writes=[("RY", m)])
                s_ = SQ[2 + m % 2]
                P.op("act", lambda e: e.activation(out=s_[:, :], in_=bank(b), func=AF.Square),
                     reads=[("ps", b)], writes=[("sq", 2 + m % 2)])

            def tail(ct):
                hc0 = U.h0 + ct * 512
                make_rstd(6, 512, D, FB[3], ("fb", 3))

                def pair(p):
                    for m in (2 * p, 2 * p + 1):
                        hk = ("h", m, hc0 // 512)
                        P.op("dve", lambda e, m=m: e.scalar_tensor_tensor(
                            out=ybuf[:, m, 0:512], in0=ybuf[:, m, 0:512], scalar=cf[:, gi, m, U.v:U.v + 1], in1=FB[3][:, :],
                            op0=ALU.mult, op1=ALU.mult), reads=[("RY", m), ("fb", 3), ("coef", l, gi)], writes=[("RY", m)])
                        P.op("dve", lambda e, m=m: e.tensor_tensor(
                            out=h[:, m, hc0:hc0 + 512], in0=h[:, m, hc0:hc0 + 512], in1=ybuf[:, m, 0:512], op=ALU.add),
                            reads=[("RY", m), hk], writes=[hk])
                if ct == 0:
                    for p in range(4):
                        pair(p)
                else:
                    for p in range(4):
                        deferred_tail.append((p, lambda p=p: pair(p)))

            A(*steps[0])
            for i, (ct, m) in enumerate(steps):
                if i + 1 < len(steps):
                    A(*steps[i + 1])
                B(ct, m)
                if m > 0:
                    ssmm(m - 1)
                yield
                if m == KC - 1:
                    ssmm(m)
                    tail(ct)

        def padded_mm(T0, U, lhs_fn, si, split=False):
            def f(e, tiles=((0, 512), (512, 512), (1024, U.nh))):
                ins = None
                for (c0, n) in tiles:
                    for kc in range(KC):
                        ins = e.matmul(ps[:, T0 * 512 + c0:T0 * 512 + c0 + n], lhs_fn(kc), u[:, kc, c0:c0 + n],
                                       start=(kc == 0), stop=(kc == KC - 1))
                return ins
            if not split:
                P.op("pe", f, reads=[("w", si)] + ALLRU, writes=[("ps", T0), ("ps", T0 + 1), ("ps", T0 + 2)])
            else:
                P.op("pe", lambda e: f(e, ((0, 512),)), reads=[("w", si)] + ALLRU, writes=[("ps", T0)])
                P.op("pe", lambda e: f(e, ((512, 512), (1024, U.nh))), reads=[("w", si)] + ALLRU, writes=[("ps", T0 + 1), ("ps", T0 + 2)])

        def tkeys(T0):
            return [("ps", T0), ("ps", T0 + 1), ("ps", T0 + 2)]

        def tflat(T0):
            return ps[:, T0 * 512:T0 * 512 + 1536]

        def conv_in_gen(U):
            if U is S0:
                save_halo()
            switch_view()
            rot = {"i": 0}

            def nextT():
                t = rot["i"]
                rot["i"] = 1 - t
                return 3 * t
            ck = lambda j, i: par[:, P_CK + j * 8 + i:P_CK + j * 8 + i + 1]
            csb = ev[1]
            r = ev[0]
            W = 1024 + U.nh
            for i in range(8):
                si = load_w(win[i], KC * 384)
                w3 = wv3(si, 384)
                Tc = nextT()
                padded_mm(Tc, U, lambda kc, w3=w3: w3[:, kc, 128:256], si)
                flush_tail()
                P.op("act", lambda e, Tc=Tc: e.activation(out=csb[:, 0:W], in_=tflat(Tc)[:, 0:W], func=AF.Copy),
                     reads=tkeys(Tc), writes=EVK[1])
                Tx = nextT()
                padded_mm(Tx, U, lambda kc, w3=w3: w3[:, kc, 256:384], si)
                P.op("dve", lambda e, Tx=Tx: e.tensor_tensor(out=csb[:, 0:W], in0=csb[:, 0:W], in1=tflat(Tx)[:, 0:W], op=ALU.mult),
                     reads=tkeys(Tx) + EVK[1], writes=EVK[1])
                Tb = nextT()
                padded_mm(Tb, U, lambda kc, w3=w3: w3[:, kc, 0:128], si)
                rc = compact(r, U)
                P.op("act", lambda e, i=i: e.activation(out=rc, in_=vcols(csb, U, 0), func=AF.Copy, scale=ck(1, i)),
                     reads=EVK[1] + ["par"], writes=EVK[0])
                P.op("dve", lambda e, i=i: e.scalar_tensor_tensor(out=rc, in0=vcols(csb, U, -1), scalar=ck(0, i), in1=rc,
                                                                  op0=ALU.mult, op1=ALU.add), reads=EVK[1] + EVK[0] + ["par"], writes=EVK[0])
                P.op("dve", lambda e, i=i: e.scalar_tensor_tensor(out=rc, in0=vcols(csb, U, 1), scalar=ck(2, i), in1=rc,
                                                                  op0=ALU.mult, op1=ALU.add), reads=EVK[1] + EVK[0] + ["par"], writes=EVK[0])
                P.op("dve", lambda e, i=i, Tb=Tb: e.tensor_tensor(out=compact(a3[:, i, :], U), in0=rc, in1=vcols(tflat(Tb), U, 0), op=ALU.mult),
                     reads=EVK[0] + tkeys(Tb), writes=[("A", i)])
                if U is S0 and i == 3:
                    load_x(2, after=[("A", 3)])
                yield

        def conv_out_gen(U):
            return proj_post_gen(U, 0, 1, KC, lambda m: wout[m],
                                 lambda k, ct: a3[:, k, ct * 512:(ct + 1) * 512], [[("A", k)] for k in range(KC)])

        def ffn_in_gen(U, l):
            if U is S0:
                save_halo()
            switch_view()
            fck = lambda j, s, c: par[:, P_FCK + l * 132 + j * 44 + s * 22 + c: P_FCK + l * 132 + j * 44 + s * 22 + c + 1]
            for c in range(NCF):
                si = load_w(wup[l, c], KC * 256)
                w3 = wv3(si, 256)
                eg = 2 * (c % 2)
                for s in range(2):
                    T0 = 3 * s
                    padded_mm(T0, U, lambda kc, w3=w3, s=s: w3[:, kc, s * 128:(s + 1) * 128], si, split=U.sample)
                    flush_tail(eg + s)
                    evk = EVK[eg + s]
                    rc = compact(ev[eg + s], U)
                    zf = tflat(T0)
                    if U.sample:
                        evb = ev[eg + s]
                        P.op("act", lambda e, evb=evb, zf=zf, s=s, c=c: e.activation(out=evb[:, 0:511], in_=zf[:, 1:512], func=AF.Copy, scale=fck(1, s, c)),
                             reads=[("ps", T0), "par"], writes=evk)
                        P.op("act", lambda e, evb=evb, zf=zf, s=s, c=c: e.activation(out=evb[:, 511:1024], in_=zf[:, 512:1025], func=AF.Copy, scale=fck(1, s, c)),
                             reads=[("ps", T0 + 1), ("ps", T0 + 2), "par"], writes=evk)
                    else:
                        P.op("act", lambda e, rc=rc, zf=zf, s=s, c=c: e.activation(out=rc, in_=vcols(zf, U, 0), func=AF.Copy, scale=fck(1, s, c)),
                             reads=tkeys(T0) + ["par"], writes=evk)
                    if s == 1:
                        P.op("act", lambda e: e.activation(out=ev[eg][:, 0:1024], in_=ev[eg][:, 0:1024], func=AF.Silu),
                             reads=EVK[eg], writes=EVK[eg])
                    P.op("dve", lambda e, rc=rc, zf=zf, s=s, c=c: e.scalar_tensor_tensor(
                        out=rc, in0=vcols(zf, U, -1), scalar=fck(0, s, c), in1=rc, op0=ALU.mult, op1=ALU.add),
                        reads=tkeys(T0) + evk + ["par"], writes=evk)
                    P.op("dve", lambda e, rc=rc, zf=zf, s=s, c=c: e.scalar_tensor_tensor(
                        out=rc, in0=vcols(zf, U, 1), scalar=fck(2, s, c), in1=rc, op0=ALU.mult, op1=ALU.add),
                        reads=tkeys(T0) + evk + ["par"], writes=evk)
                    if s == 1:
                        P.op("dve", lambda e, c=c: e.tensor_tensor(out=a3[:, c, :], in0=ev[eg][:, 0:1024], in1=ev[eg + 1][:, 0:1024], op=ALU.mult),
                             reads=EVK[eg] + EVK[eg + 1], writes=[("A", c)])
                yield

        def ffn_out_gen(U, l):
            return proj_post_gen(U, l, 3, NCF, lambda m: wdn[l, m],
                                 lambda k, ct: a3[:, k, ct * 512:(ct + 1) * 512], [[("A", k)] for k in range(NCF)])

        qb = a3[:, 0:8, :]
        kt = ra_t[:, 8 * 1024:8 * 1024 + 2 * NKEY].rearrange("p (j s) -> p j s", j=2)
        vb = ra_t[:, 13 * 1024:13 * 1024 + 20 * 256].rearrange("p (c n) -> p c n", n=256)
        tab = ra_t[:, 18 * 1024:22 * 1024].bitcast(F32).rearrange("p (a t) -> p a t", a=2)
        KA = [("A", c) for c in range(8, 13)]
        VA = [("A", c) for c in range(13, 18)]
        TA = [("A", c) for c in range(18, 22)]
        att_scale = float(128 ** -0.5)
        qg = par[:, P_QG:P_QG + 1]
        kg = par[:, P_KG:P_KG + 1]

        def attn_setup():
            P.dma("pool", kt[:, :, 0:PAST], ckT, writes=["KT"] + KA)
            P.dma("pool", vb[:, 0:4, :], cvl, writes=["V"] + VA)

        def qk_chain(U, si, w3, ct, slot, dst, dkeys, gain, stage_out):
            b = nextbank(4)
            sqb, sqk = SQ[slot], ("sq", slot)
            X, Xk = evq[2 * slot], EVQK[2 * slot]
            Y, Yk = evq[2 * slot + 1], EVQK[2 * slot + 1]
            ssb = 4 + slot
            rtb = b

            def f(e):
                if U.sample:
                    for (u0, n, off) in U.vt[ct]:
                        for kc in range(KC):
                            ins = e.matmul(bank(b, n, off), w3[:, kc, :], u[:, kc, u0:u0 + n], start=(kc == 0), stop=(kc == KC - 1))
                else:
                    o3 = bank(b).rearrange("p (a n) -> p a n", a=2)
                    for kc in range(KC):
                        r3 = u[:, kc, 0:UW].rearrange("p (i w) -> p i w", w=258)[:, 2 * ct:2 * ct + 2, 1:257]
                        ins = e.matmul(o3, w3[:, kc, :], r3, start=(kc == 0), stop=(kc == KC - 1))
                return ins
            P.op("pe", f, reads=[("w", si)] + ALLRU, writes=[("ps", b)])
            yield
            P.op("act", lambda e: e.activation(out=sqb[:, :], in_=bank(b), func=AF.Square), reads=[("ps", b)], writes=[sqk])
            P.op("pe", lambda e: e.matmul(bank(ssb), ones, sqb[:, :], start=True, stop=True), reads=[sqk, "ones"], writes=[("ps", ssb)])
            yield
            make_rstd(ssb, 512, 128, X, Xk)
            yield
            if not U.sample:
                if stage_out is None:
                    P.op("dve", lambda e: e.scalar_tensor_tensor(out=dst, in0=bank(b), scalar=gain, in1=X[:, :], op0=ALU.mult, op1=ALU.mult),
                         reads=[("ps", b), Xk, "par"], writes=dkeys)
                else:
                    P.op("dve", lambda e: e.scalar_tensor_tensor(out=Y[:, :], in0=bank(b), scalar=gain, in1=X[:, :], op0=ALU.mult, op1=ALU.mult),
                         reads=[("ps", b), Xk, "par"], writes=[Yk])
                    P.op("act", lambda e: e.activation(out=dst, in_=Y[:, :], func=AF.Copy), reads=[Yk], writes=dkeys)
                    P.dma("sp", stage_out, Y[:, :], reads=[Yk], is_output=True)
                yield
            else:
                P.op("dve", lambda e: e.scalar_tensor_tensor(out=Y[:, :], in0=bank(b), scalar=gain, in1=X[:, :], op0=ALU.mult, op1=ALU.mult),
                     reads=[("ps", b), Xk, "par"], writes=[Yk])
                P.op("pe", lambda e: e.matmul(bank(rtb), rrot_t[:], Y[:, :], start=True, stop=True), reads=[Yk, "rrot"], writes=[("ps", rtb)])
                yield
                P.op("dve", lambda e: e.tensor_tensor(out=X[:, :], in0=Y[:, :], in1=tab[:, 0, ct * 512:(ct + 1) * 512], op=ALU.mult),
                     reads=[Yk, "tab"] + TA, writes=[Xk])
                yield
                P.op("dve", lambda e: e.tensor_tensor(out=Y[:, :], in0=bank(rtb), in1=tab[:, 1, ct * 512:(ct + 1) * 512], op=ALU.mult),
                     reads=[("ps", rtb), "tab"] + TA, writes=[Yk])
                yield
                P.op("dve", lambda e: e.tensor_tensor(out=dst, in0=Y[:, :], in1=X[:, :], op=ALU.add), reads=[Yk, Xk], writes=dkeys)
                yield

        def v_gen(U):
            si = load_w(wv, KC * 256)
            w3 = wv3(si, 256)
            for tt in range(8):
                if U.sample:
                    u0 = 1 + tt * 128
                    vch = 4 + U.h0 // 128 + tt
                else:
                    u0 = 1 + 258 * (tt // 2) + 128 * (tt % 2)
                    vch = tt
                b = nextbank(4)

                def f(e, b=b, u0=u0):
                    for kc in range(KC):
                        ins = e.matmul(bank(b, 256), u[:, kc, u0:u0 + 128], w3[:, kc, :], start=(kc == 0), stop=(kc == KC - 1))
                    return ins
                P.op("pe", f, reads=[("w", si)] + ALLRU, writes=[("ps", b)])
                P.op("act", lambda e, b=b, vch=vch: e.activation(out=vb[:, vch, :], in_=bank(b, 256), func=AF.Copy),
                     reads=[("ps", b)], writes=["V"] + VA)
                if not U.sample:
                    t, tk = evq[tt % 2 * 2 + 1], EVQK[tt % 2 * 2 + 1]
                    P.op("act", lambda e, b=b, t=t: e.activation(out=t[:, 0:256], in_=bank(b, 256), func=AF.Copy), reads=[("ps", b)], writes=[tk])
                    P.dma("sp", v_out[tt * 128:(tt + 1) * 128, :], t[:, 0:256], reads=[tk], is_output=True)
                yield

        def attn_in_gen(U, kv, q):
            flush_tail()
            if U.sample:
                for a in range(2):
                    P.dma("sp", tab[:, a, :], rope[a, :, U.h0:U.h0 + 1024], writes=["tab"] + TA)
            chains = []
            heads = ([8, 9] if kv else []) + (list(range(8)) if q else [])
            kbase = (PAST + U.h0) if U.sample else 0
            active = []
            slot_free = [0, 1, 2, 3]
            if kv:
                yield from v_gen(U)
            for hd in heads:
                si = load_w(wqk[hd], KC * 128)
                w3 = wv3(si, 128)
                for ct in range(2):
                    if hd >= 8:
                        j = hd - 8
                        dst = kt[:, j, kbase + ct * 512:kbase + (ct + 1) * 512]
                        dkeys = ["QK"] + KA
                        so = None if U.sample else k_out[:, j, ct * 512:(ct + 1) * 512]
                        gain = kg
                    else:
                        dst = qb[:, hd, ct * 512:(ct + 1) * 512]
                        dkeys = [("A", hd), ("Aq", hd, ct * 512), ("Aq", hd, ct * 512 + 256)]
                        so = None
                        gain = qg
                    while not slot_free:
                        for (g_, s_) in list(active):
                            try:
                                next(g_)
                            except StopIteration:
                                active.remove((g_, s_))
                                slot_free.append(s_)
                        yield
                    s_ = slot_free.pop(0)
                    g_ = qk_chain(U, si, w3, ct, s_, dst, dkeys, gain, so)
                    next(g_)
                    active.append((g_, s_))
            while active:
                for (g_, s_) in list(active):
                    try:
                        next(g_)
                    except StopIteration:
                        active.remove((g_, s_))
                yield

        def attend_gen(U):
            grp = {"i": 0}
            sbk = {"i": 0}
            pti = {"i": 0}
            if U.sample:
                jobs = [(hd, 1, qt * 512, 512, [(c * 128, c) for c in range(20)]) for hd in range(8) for qt in range(2)]
            else:
                jobs = [(hp * 2, 2, sq * 256, 256, [(sq * 256 + c * 128, 2 * sq + c) for c in range(2)])
                        for sq in range(4) for hp in range(4)]
            pending_fin = []
            for (hd, nh, q0, nq, keys) in jobs:
                j = hd // 4
                g = grp["i"]
                grp["i"] = 1 - g
                ob, db = 0 + g, 2 + g
                pend = []
                NQ = nh * nq
                qkeys = [("Aq", hd + a, q0 + c0) for a in range(nh) for c0 in ((0, 256) if nq == 512 else (0,))]
                q_ap = qb[:, hd:hd + nh, q0:q0 + nq]

                def v3(ap2d):
                    return ap2d.rearrange("p (a n) -> p a n", a=nh)

                def pv(item, first, last):
                    (pi, vch) = item
                    P.op("pe", lambda e: e.matmul(bank(ob, NQ), vb[:, vch, j * 128:(j + 1) * 128], pt[pi][:, 0:NQ], start=first, stop=last),
                         reads=[PTK[pi], "V"] + VA, writes=[("ps", ob)])
                    P.op("pe", lambda e: e.matmul(bank(db, NQ), ones, pt[pi][:, 0:NQ], start=first, stop=last),
                         reads=[PTK[pi], "ones"], writes=[("ps", db)])
                done = 0
                for ki, (k0, vch) in enumerate(keys):
                    sbank = 4 + sbk["i"]
                    sbk["i"] = (sbk["i"] + 1) % 3
                    pi = pti["i"]
                    pti["i"] = (pi + 1) % 4
                    P.op("pe", lambda e: e.matmul(v3(bank(sbank, NQ)), kt[:, j, k0:k0 + 128], q_ap, start=True, stop=True),
                         reads=["KT", "QK"] + qkeys + KA, writes=[("ps", sbank)])
                    P.op("act", lambda e: e.activation(out=pt[pi][:, 0:NQ], in_=bank(sbank, NQ), func=AF.Exp, scale=att_scale),
                         reads=[("ps", sbank)], writes=[PTK[pi]])
                    pend.append((pi, vch))
                    if len(pend) > 2:
                        pv(pend.pop(0), done == 0, False)
                        done += 1
                    if ki == min(1, len(keys) - 1) and pending_fin:
                        pending_fin.pop(0)()
                    yield
                while pend:
                    it = pend.pop(0)
                    pv(it, done == 0, len(pend) == 0)
                    done += 1
                def fin(ob=ob, db=db, NQ=NQ, q_ap=q_ap, qkeys=qkeys, v3=v3):
                    if U.sample:
                        P.op("dve", lambda e: e.reciprocal(out=FB[3][:, 0:NQ], in_=bank(db, NQ)), reads=[("ps", db)], writes=[("fb", 3)])
                    else:
                        P.op("act", lambda e: e.activation(out=FB[3][:, 0:NQ], in_=bank(db, NQ), func=AF.Ln), reads=[("ps", db)], writes=[("fb", 3)])
                        P.op("act", lambda e: e.activation(out=FB[3][:, 0:NQ], in_=FB[3][:, 0:NQ], func=AF.Exp, scale=-1.0), reads=[("fb", 3)], writes=[("fb", 3)])
                    P.op("dve", lambda e: e.tensor_tensor(out=q_ap, in0=v3(bank(ob, NQ)), in1=v3(FB[3][:, 0:NQ]), op=ALU.mult),
                         reads=[("ps", ob), ("fb", 3)], writes=qkeys)
                pending_fin.append(fin)
                yield
            while pending_fin:
                pending_fin.pop(0)()

        def attn_out_gen(U):
            yield from attend_gen(U)
            yield from proj_post_gen(U, 1, 1, KC, lambda m: wo[m],
                                     lambda k, ct: qb[:, k, ct * 512:(ct + 1) * 512],
                                     [[("A", k)] + [("Aq", k, c0) for c0 in (0, 256, 512, 768)] for k in range(KC)], ystart=4)

        class Stage:
            def __init__(self, norm, inn, out, per=0.2):
                self.norm, self.inn, self.out, self.per = norm, inn, out, per
                self.norm_done = False

        def chain(*gens):
            for g_ in gens:
                yield from g_

        stages = []
        for U in units:
            inn = (lambda U=U: conv_in_gen(U))
            if U is S0:
                inn = (lambda U=U: weave_gen(conv_in_gen(U), mod_gen(0, [2, 3]), 0.8))
            if U is S1:
                inn = (lambda U=U: weave_gen(conv_in_gen(U), mod_gen(0, [4]), 1.6))
            if U is PU:
                inn = (lambda U=U: weave_gen(conv_in_gen(U), mod_gen(0, [5]), 1.6))
            stages.append(Stage(lambda U=U: norm_gen(U, 0, 0, 0, skip_first_stats=(U is S0)), inn, lambda U=U: conv_out_gen(U)))
        for U in units:
            inn = (lambda U=U: ffn_in_gen(U, 0))
            if U is S1:
                inn = (lambda U=U: weave_gen(ffn_in_gen(U, 0), mod_gen(1, [0, 1, 2]), 1.4))
            if U is PU:
                inn = (lambda U=U: weave_gen(ffn_in_gen(U, 0), mod_gen(1, [3, 4, 5]), 1.4))
            stages.append(Stage(lambda U=U: norm_gen(U, 0, 2, 3), inn, lambda U=U: ffn_out_gen(U, 0)))
        stages.append(Stage(lambda: norm_gen(S0, 1, 0, 0, halo=False), lambda: chain_setup(attn_in_gen(S0, True, False)), None))
        stages.append(Stage(lambda: norm_gen(S1, 1, 0, 0, halo=False), lambda: attn_in_gen(S1, True, True), lambda: attn_out_gen(S1), per=6.0))
        stages.append(Stage(lambda: norm_gen(S0, 1, 0, 0, halo=False), lambda: attn_in_gen(S0, False, True), lambda: attn_out_gen(S0), per=4.4))
        stages.append(Stage(lambda: norm_gen(PU, 1, 0, 0, halo=False), lambda: chain(attn_in_gen(PU, True, True), attend_gen(PU)),
                            lambda: proj_post_gen(PU, 1, 1, KC, lambda m: wo[m], lambda k, ct: qb[:, k, ct * 512:(ct + 1) * 512],
                                                  [[("A", k)] + [("Aq", k, c0) for c0 in (0, 256, 512, 768)] for k in range(KC)], ystart=4)))
        for U in units:
            stages.append(Stage(lambda U=U: norm_gen(U, 1, 2, 3), lambda U=U: ffn_in_gen(U, 1), lambda U=U: ffn_out_gen(U, 1)))

        def weave_gen(main, side, per):
            side = iter(side)
            credit = 0.0
            s_alive = True
            for _ in main:
                credit += 1.0
                while s_alive and credit >= per:
                    credit -= per
                    try:
                        next(side)
                    except StopIteration:
                        s_alive = False
                yield
            if s_alive:
                for _ in side:
                    yield

        def chain_setup(g_):
            yield from g_
            attn_setup()

        run(norm_gen(S0, 0, 0, 0, stats_only_first=True))
        run(mod_gen(0, [0, 1]))
        load_x(1, after=[("coef", 0, 0)])
        for i, st in enumerate(stages):
            if not st.norm_done:
                flush_tail()
                run(st.norm())
            run(st.inn())
            nxt = stages[i + 1] if i + 1 < len(stages) else None
            if st.out is not None:
                if nxt is not None:
                    weave(st.out(), nxt.norm(), st.per)
                    nxt.norm_done = True
                else:
                    run(st.out())
        flush_tail()
        for ui in range(3):
            for kc in range(KC):
                P.dma("sp", h_out[:, kc, ui * 1024:(ui + 1) * 1024], h[:, kc, ui * 1024:(ui + 1) * 1024],
                      reads=[("h", kc, 2 * ui), ("h", kc, 2 * ui + 1)], is_output=True)
        P.finish()
    return nc


def _fm(w):
    return w.reshape(KC, 128, w.shape[1]).transpose(1, 0, 2)


def _host_layout(inp):
    f = lambda a: np.ascontiguousarray(a, dtype=np.float32)
    shared = {}
    mod_w = inp["mod_w"]
    shared["modw"] = f(np.stack([np.stack([_fm(mod_w[l][:, blk * 256:(blk + 1) * 256]).reshape(128, KC * 256)
                                           for blk in range(24)]) for l in range(2)]))
    w_in = inp["conv_w_in"][0]
    shared["win"] = f(np.stack([np.concatenate([_fm(w_in[:, s * 1024 + i * 128: s * 1024 + (i + 1) * 128]) for s in range(3)], axis=2)
                                .reshape(128, KC * 384) for i in range(8)]))
    w_o1 = inp["conv_w_out"][0]
    shared["wout"] = f(np.stack([_fm(w_o1[:, m * 128:(m + 1) * 128]).reshape(128, KC * 128) for m in range(8)]))
    wqkv = inp["attn_w_qkv"][0]
    shared["wqk"] = f(np.stack([_fm(wqkv[:, hd * 128:(hd + 1) * 128]).reshape(128, KC * 128) for hd in range(10)]))
    shared["wv"] = f(_fm(wqkv[:, 1280:1536]).reshape(128, KC * 256))
    w_o2 = inp["attn_w_o"][0]
    shared["wo"] = f(np.stack([_fm(w_o2[:, m * 128:(m + 1) * 128]).reshape(128, KC * 128) for m in range(8)]))
    wup = inp["ffn_w_up"]
    shared["wup"] = f(np.stack([np.stack([np.concatenate([_fm(wup[l][:, s * DFF + c * 128: s * DFF + (c + 1) * 128]) for s in range(2)], axis=2)
                                          .reshape(128, KC * 256) for c in range(NCF)]) for l in range(2)]))
    wdn = inp["ffn_w_down"]
    shared["wdn"] = f(np.stack([np.stack([wdn[l][:, m * 128:(m + 1) * 128].reshape(NCF, 128, 128).transpose(1, 0, 2).reshape(128, NCF * 128)
                                          for m in range(8)]) for l in range(2)]))
    t = np.arange(TS)
    row = (t // 64).astype(np.float32)
    col = (t % 64).astype(np.float32)
    inv = (10000.0 ** (-np.arange(0, 64, 2, dtype=np.float32) / 64)).astype(np.float32)
    cos = np.zeros((128, TS), np.float32)
    sin = np.zeros((128, TS), np.float32)
    for d in range(128):
        pos = row if d < 64 else col
        ang = (pos * inv[d % 32]).astype(np.float32)
        cos[d] = np.cos(ang)
        sin[d] = np.sin(ang)
    shared["rope"] = f(np.stack([cos, sin]))
    R = np.zeros((128, 128), np.float32)
    for d in range(128):
        blk = (d // 64) * 64
        dd = d % 64
        if dd < 32:
            R[blk + dd + 32, d] = -1.0
        else:
            R[blk + dd - 32, d] = 1.0
    shared["rrot"] = R
    par = np.zeros((128, P_N), np.float32)
    gains = [inp["norm_mix_pre"], inp["norm_mix_post"], inp["norm_ffn_pre"], inp["norm_ffn_post"]]
    for l in range(2):
        for wi, g in enumerate(gains):
            par[:, P_GAIN + l * 32 + wi * 8: P_GAIN + l * 32 + wi * 8 + 8] = g[l].reshape(KC, 128).T
        par[:, P_MODB + l * 48: P_MODB + (l + 1) * 48] = inp["mod_b"][l].reshape(48, 128).T
        fc = inp["ffn_conv"][l]
        for j in range(3):
            par[:, P_FCK + l * 132 + j * 44: P_FCK + l * 132 + (j + 1) * 44] = fc[j].reshape(44, 128).T
    ck = inp["conv_k"][0]
    for j in range(3):
        par[:, P_CK + j * 8: P_CK + (j + 1) * 8] = ck[j].reshape(8, 128).T
    par[:, P_QG] = inp["attn_q_gain"][0]
    par[:, P_KG] = inp["attn_k_gain"][0]
    par[:, P_EPS] = EPS
    cctx = inp["c_ctx"].reshape(KC, 128).T
    in_maps = []
    for b in range(N_CORES):
        m = dict(shared)
        p = par.copy()
        cv = np.stack([inp["c"][b].reshape(KC, 128).T, cctx], axis=2)
        p[:, P_CVEC:P_CVEC + 16] = cv.reshape(128, 16)
        m["params"] = p
        xs = inp["x_sample"][b]
        xp = inp["x_prompt"][4 * b:4 * b + 4].reshape(TP, D)
        xa = np.concatenate([xs, xp], axis=0)
        m["x_in"] = f(xa.reshape(TT, KC, 128).transpose(2, 1, 0))
        m["ckT"] = f(inp["cache_k"][b, 0].transpose(2, 1, 0))
        m["cvl"] = f(inp["cache_v"][b, 0].reshape(4, 128, 256).transpose(1, 0, 2))
        in_maps.append(m)
    return in_maps


_NC_CACHE = {}


def kernel(**inputs):
    inp = {k: np.asarray(v) for k, v in inputs.items()}
    in_maps = _host_layout(inp)
    if "nc" not in _NC_CACHE:
        _NC_CACHE["nc"] = build_program()
    nc = _NC_CACHE["nc"]
    res = run_bass_kernel_spmd(nc, in_maps, core_ids=list(range(N_CORES)))
    y_prompt = np.zeros((32, 256, D), np.float32)
    y_sample = np.zeros((8, TS, D), np.float32)
    new_k = np.zeros((32, 1, 256, 2, 128), np.float32)
    new_v = np.zeros((32, 1, 256, 2, 128), np.float32)
    for b in range(N_CORES):
        r = res.results[b]
        ho = np.asarray(r["h_out"]).transpose(2, 1, 0).reshape(TT, D)
        y_sample[b] = ho[:TS]
        y_prompt[4 * b:4 * b + 4] = ho[TS:].reshape(4, 256, D)
        ko = np.asarray(r["k_out"]).transpose(2, 1, 0)
        new_k[4 * b:4 * b + 4, 0] = ko.reshape(4, 256, 2, 128)
        vo = np.asarray(r["v_out"]).reshape(4, 256, 2, 128)
        new_v[4 * b:4 * b + 4, 0] = vo
    return (y_prompt, y_sample, new_k, new_v)
```

```python
import numpy as np
from contextlib import ExitStack
import concourse.bass as bass
import concourse.mybir as mybir
from concourse.bass_utils import run_bass_kernel_spmd

F32 = mybir.dt.float32
BF16 = mybir.dt.bfloat16
ALU = mybir.AluOpType
AF = mybir.ActivationFunctionType

D = 1024
KC = 8
DFF = 2816
NCF = 22
TS = 2048
TP = 1024
TT = TS + TP
PAST = 512
NKEY = PAST + TS
EPS = 1e-6
UW = 1032
N_CORES = 8

P_GAIN = 0
P_MODB = 64
P_CK = P_MODB + 96
P_FCK = P_CK + 24
P_QG = P_FCK + 264
P_KG = P_QG + 1
P_CVEC = P_KG + 1
P_EPS = P_CVEC + 16
P_N = P_EPS + 1


class Prog:
    def __init__(self, nc, es):
        self.nc = nc
        self.engs = {"pe": nc.tensor, "act": nc.scalar, "dve": nc.vector, "pool": nc.gpsimd, "sp": nc.sync}
        self.sems = {}
        self.cnt = {}
        for e in self.engs:
            self.sems[e] = es.enter_context(nc.semaphore("s_" + e))
            self.cnt[e] = 0
        self.dsem = {}
        self.dcnt = {}
        self.dnext = {"pool": 0, "sp": 0}
        self.ND = 12
        for q in ("pool", "sp"):
            for i in range(self.ND):
                k = "d_%s%d" % (q, i)
                self.sems[k] = es.enter_context(nc.semaphore(k))
                self.cnt[k] = 0
        self.waited = {}
        self.lastw = {}
        self.readers = {}
        self.out_events = []

    def _need(self, reads, writes):
        need = {}

        def add(ev):
            if ev is None:
                return
            k, v = ev
            if need.get(k, 0) < v:
                need[k] = v
        for r in reads:
            add(self.lastw.get(r))
            if isinstance(r, tuple) and r[0] == "ps":
                for ev in self.readers.get(r, ()):
                    add(ev)
        for w in writes:
            add(self.lastw.get(w))
            for ev in self.readers.get(w, ()):
                add(ev)
        return need

    def _emit_waits(self, e, need):
        eng = self.engs[e]
        for k, v in need.items():
            if e == "pe" and k == "pe":
                continue
            if self.waited.get((e, k), 0) < v:
                eng.wait_ge(self.sems[k], v)
                self.waited[(e, k)] = v

    def _record(self, ev, reads, writes):
        for r in reads:
            self.readers.setdefault(r, []).append(ev)
        for w in writes:
            self.lastw[w] = ev
            self.readers[w] = []

    def op(self, e, fn, reads=(), writes=()):
        need = self._need(reads, writes)
        self._emit_waits(e, need)
        ins = fn(self.engs[e])
        self.cnt[e] += 1
        ins.then_inc(self.sems[e], 1)
        ev = (e, self.cnt[e])
        self.waited[(e, e)] = max(self.waited.get((e, e), 0), 0)
        self._record(ev, reads, writes)
        return ev

    def dma(self, q, out, in_, reads=(), writes=(), is_output=False):
        i = self.dnext[q]
        self.dnext[q] = (i + 1) % self.ND
        k = "d_%s%d" % (q, i)
        need = self._need(reads, writes)
        if self.cnt[k] > 0:
            need[k] = max(need.get(k, 0), self.cnt[k])
        self._emit_waits(q, need)
        self.engs[q].dma_start(out=out, in_=in_).then_inc(self.sems[k], 16)
        self.cnt[k] += 16
        ev = (k, self.cnt[k])
        self._record(ev, reads, writes)
        if is_output:
            self.out_events.append(ev)
        return ev

    def barrier(self, engs=("pe", "act", "dve")):
        for e in engs:
            need = {p: self.cnt[p] for p in engs if self.cnt[p] > 0}
            self._emit_waits(e, need)

    def finish(self):
        need = {}
        for k, v in self.out_events:
            need[k] = max(need.get(k, 0), v)
        self._emit_waits("sp", need)


def run(gen):
    for _ in gen:
        pass


def weave(main, side, per):
    main = iter(main)
    side = iter(side)
    credit = 0.0
    m_alive = True
    s_alive = True
    while m_alive:
        try:
            next(main)
        except StopIteration:
            m_alive = False
            break
        credit += 1.0
        while s_alive and credit >= per:
            credit -= per
            try:
                next(side)
            except StopIteration:
                s_alive = False
    if s_alive:
        for _ in side:
            pass


def build_program():
    nc = bass.Bass("TRN2", target_bir_lowering=False)

    def din(name, shape):
        return nc.dram_tensor(name, list(shape), F32, kind="ExternalInput").ap()

    def dout(name, shape):
        return nc.dram_tensor(name, list(shape), F32, kind="ExternalOutput").ap()

    x_in = din("x_in", [128, KC, TT])
    params = din("params", [128, P_N])
    modw = din("modw", [2, 24, 128, KC * 256])
    win = din("win", [8, 128, KC * 384])
    wout = din("wout", [8, 128, KC * 128])
    wqk = din("wqk", [10, 128, KC * 128])
    wv = din("wv", [128, KC * 256])
    wo = din("wo", [8, 128, KC * 128])
    wup = din("wup", [2, NCF, 128, KC * 256])
    wdn = din("wdn", [2, 8, 128, NCF * 128])
    ckT = din("ckT", [128, 2, PAST])
    cvl = din("cvl", [128, 4, 256])
    rope = din("rope", [2, 128, TS])
    rrot = din("rrot", [128, 128])
    h_out = dout("h_out", [128, KC, TT])
    k_out = dout("k_out", [128, 2, TP])
    v_out = dout("v_out", [TP, 256])

    with ExitStack() as es:
        P = Prog(nc, es)

        def sb(name, cols, dt):
            return es.enter_context(nc.sbuf_tensor(name, [128, cols], dt))

        h_t = sb("h", KC * TT, F32)
        ru_t = sb("ru", KC * UW, BF16)
        ry_t = sb("ry", KC * 516, F32)
        ra_t = sb("ra", NCF * 1024, BF16)
        wsl_t = [sb("wsl%d" % i, 3072, BF16) for i in range(3)]
        SQ = [sb("sq%d" % i, 512, BF16) for i in range(4)]
        FB = [sb("fb%d" % i, 512, F32) for i in range(4)]
        par_t = sb("par", P_N, F32)
        ones_t = sb("ones", 128, BF16)
        rrot_t = sb("rrotsb", 128, F32)
        silu_t = sb("silu", 16, BF16)
        modT_t = sb("modT", 2 * 96, F32)
        coef_t = sb("coef", 2 * 64, F32)
        hsave_t = sb("hsave", KC, F32)
        dummy_t = sb("dmy_scr", 2, F32)
        ps_t = es.enter_context(nc.psum_tensor("ps", [128, 4096], F32))

        h = h_t[:].rearrange("p (k t) -> p k t", k=KC)
        u = ru_t[:].rearrange("p (k t) -> p k t", k=KC)
        ybuf = ry_t[:].rearrange("p (k t) -> p k t", k=KC)
        evq = [ry_t[:, 516 * i:516 * i + 512] for i in range(8)]
        EVQK = [("RY", i) for i in range(8)]
        ev = [ry_t[:, i * 1032:(i + 1) * 1032] for i in range(4)]
        EVK = [[("RY", 2 * i), ("RY", 2 * i + 1)] for i in range(4)]
        pt = [ry_t[:, 516 * (4 + i):516 * (4 + i) + 256].bitcast(BF16) for i in range(4)]
        PTK = [("RY", 4 + i) for i in range(4)]
        a3 = ra_t[:].rearrange("p (c t) -> p c t", c=NCF)
        ps = ps_t[:]
        par = par_t[:]
        ones = ones_t[:]

        def bank(b, n=512, off=0):
            return ps[:, b * 512 + off: b * 512 + off + n]

        def switch_view():
            pass

        P.dma("sp", par, params, writes=["par"])
        def load_x(ui, after=()):
            for kc in range(KC):
                P.dma("sp", h[:, kc, ui * 1024:(ui + 1) * 1024], x_in[:, kc, ui * 1024:(ui + 1) * 1024],
                      reads=list(after), writes=[("h", kc, 2 * ui), ("h", kc, 2 * ui + 1)])
        load_x(0)
        P.dma("sp", rrot_t[:], rrot, writes=["rrot"])
        P.op("dve", lambda e: e.memset(ones, 1.0), writes=["ones"])
        P.op("act", lambda e: e.activation(out=silu_t[:], in_=par[:, P_CVEC:P_CVEC + 16], func=AF.Silu),
             reads=["par"], writes=["silu"])
        eps_ap = par[:, P_EPS:P_EPS + 1]

        wstate = {"i": 0}

        def load_w(src, ncols):
            i = wstate["i"]
            wstate["i"] = (i + 1) % 3
            P.dma("pool", wsl_t[i][:, 0:ncols], src, writes=[("w", i)])
            return i

        def wv3(i, inner, nk=KC):
            return wsl_t[i][:, 0:nk * inner].rearrange("p (k n) -> p k n", n=inner)

        mvs = [modT_t[:, l * 96:(l + 1) * 96].rearrange("p (c v) -> p c v", v=2) for l in range(2)]
        cfs = [coef_t[:, l * 64:(l + 1) * 64].rearrange("p (a k v) -> p a k v", a=4, k=KC) for l in range(2)]

        def mod_gen(l, parts):
            mv, cf = mvs[l], cfs[l]
            g = lambda which: par[:, P_GAIN + l * 32 + which * 8: P_GAIN + l * 32 + which * 8 + 8]
            for part in parts:
                for bi in range(4):
                    blk = part * 4 + bi
                    si = load_w(modw[l, blk], KC * 256)
                    w3 = wv3(si, 256)
                    for j in range(2):
                        ch = blk * 2 + j

                        def f(e, w3=w3, j=j, ch=ch):
                            for kc in range(KC):
                                ins = e.matmul(bank(6, 2, ch * 2), w3[:, kc, j * 128:(j + 1) * 128],
                                               silu_t[:, kc * 2:kc * 2 + 2], start=(kc == 0), stop=(kc == KC - 1))
                            return ins
                        P.op("pe", f, reads=[("w", si), "silu"], writes=[("ps", 6)])
                    yield
                modb = par[:, P_MODB + l * 48 + part * 8:P_MODB + l * 48 + part * 8 + 8]
                psv = bank(6, 16, part * 16).rearrange("p (c v) -> p c v", v=2)
                for v in range(2):
                    P.op("dve", lambda e, v=v: e.tensor_tensor(out=mv[:, part * 8:(part + 1) * 8, v], in0=psv[:, :, v], in1=modb, op=ALU.add),
                         reads=[("ps", 6), "par"], writes=[("modT", l, part)])
                spec = {1: (0, 0, True), 2: (1, 1, False), 4: (2, 2, True), 5: (3, 3, False)}.get(part)
                if spec is not None:
                    ai, which, isA = spec
                    for v in range(2):
                        src = mv[:, part * 8:(part + 1) * 8, v]
                        if isA:
                            P.op("dve", lambda e, v=v, src=src: e.scalar_tensor_tensor(
                                out=cf[:, ai, :, v], in0=src, scalar=1.0, in1=g(which), op0=ALU.add, op1=ALU.mult),
                                reads=[("modT", l, part), "par"], writes=[("coef", l, ai)])
                        else:
                            P.op("dve", lambda e, v=v, src=src: e.tensor_tensor(
                                out=cf[:, ai, :, v], in0=src, in1=g(which), op=ALU.mult),
                                reads=[("modT", l, part), "par"], writes=[("coef", l, ai)])
                yield

        class Unit:
            pass
        units = []
        for ui, name in enumerate(("S0", "S1", "P")):
            U = Unit()
            U.name = name
            U.h0 = ui * 1024
            U.v = 0 if ui < 2 else 1
            U.sample = ui < 2
            if name == "S0":
                U.ntiles = [(0, 512, 1), (512, 512, 513), (1024, 1, 1025)]
                U.zero = [(0, 1)]
                U.nh = 2
            elif name == "S1":
                U.ntiles = [(1024, 512, 1), (1536, 512, 513), (1023, 1, 0)]
                U.zero = [(1025, 1)]
                U.nh = 2
            else:
                U.ntiles = [(2048 + 256 * i, 256, 1 + 258 * i) for i in range(4)]
                U.zero = [(0, 1), (257, 2), (515, 2), (773, 2), (1031, 1)]
                U.nh = 8
            if U.sample:
                U.vt = [[(1, 512, 0)], [(513, 512, 0)]]
            else:
                U.vt = [[(1, 256, 0), (259, 256, 256)], [(517, 256, 0), (775, 256, 256)]]
            units.append(U)
        S0, S1, PU = units

        def vcols(flat, U, shift):
            if U.sample:
                return flat[:, 1 + shift:1025 + shift]
            return flat[:, 0:1032].rearrange("p (i w) -> p i w", w=258)[:, :, 1 + shift:257 + shift]

        def compact(flat, U):
            if U.sample:
                return flat[:, 0:1024]
            return flat[:, 0:1024].rearrange("p (i w) -> p i w", w=256)

        ALLRU = [("RU", kc) for kc in range(KC)]

        def make_rstd(ss_bank, n, dim, dst, dkey):
            P.op("act", lambda e: e.activation(out=dst[:, 0:n], in_=bank(ss_bank, n), func=AF.Ln, bias=eps_ap, scale=1.0 / dim),
                 reads=[("ps", ss_bank), "par"], writes=[dkey])
            P.op("act", lambda e: e.activation(out=dst[:, 0:n], in_=dst[:, 0:n], func=AF.Exp, scale=-0.5),
                 reads=[dkey], writes=[dkey])

        def save_halo():
            P.op("dve", lambda e: e.tensor_copy(out=hsave_t[:, :], in_=h[:, :, 1023]),
                 reads=[("h", kc, 1) for kc in range(KC)], writes=["hsave"])

        def norm_gen(U, l, ai, bpart, halo=True, stats_only_first=False, skip_first_stats=False):
            mv, cf = mvs[l], cfs[l]
            if not stats_only_first:
                for (c0, w) in U.zero:
                    P.op("dve", lambda e, c0=c0, w=w: e.memset(u[:, :, c0:c0 + w], 0.0), writes=ALLRU)
            for ti, (h0, n, u0) in enumerate(U.ntiles):
                if n == 1 and not halo:
                    continue
                if stats_only_first and ti > 0:
                    return
                hk = [("h", kc, h0 // 512) for kc in range(KC)]
                if U.name == "S1" and n == 1:
                    hk = ["hsave"] * KC
                    hsrc = lambda kc: hsave_t[:, kc:kc + 1]
                else:
                    hsrc = lambda kc, h0=h0, n=n: h[:, kc, h0:h0 + n]
                def sqop(kc):
                    s = SQ[kc % 2]
                    P.op("act", lambda e: e.activation(out=s[:, 0:n], in_=hsrc(kc), func=AF.Square),
                         reads=[hk[kc]], writes=[("sq", kc % 2)])

                def mmop(kc):
                    s = SQ[kc % 2]
                    P.op("pe", lambda e: e.matmul(bank(7, n), ones, s[:, 0:n], start=(kc == 0), stop=(kc == KC - 1)),
                         reads=[("sq", kc % 2), "ones"], writes=[("ps", 7)])
                if not (skip_first_stats and ti == 0):
                    sqop(0)
                    sqop(1)
                    yield
                    for kc in range(KC):
                        mmop(kc)
                        if kc + 2 < KC:
                            sqop(kc + 2)
                        yield
                    make_rstd(7, n, D, FB[0], ("fb", 0))
                    yield
                if stats_only_first:
                    return
                for kc in range(KC):
                    t = FB[1 + kc % 2]
                    tk = ("fb", 1 + kc % 2)
                    P.op("dve", lambda e, kc=kc, t=t: e.scalar_tensor_tensor(
                        out=t[:, 0:n], in0=hsrc(kc), scalar=cf[:, ai, kc, U.v:U.v + 1], in1=FB[0][:, 0:n],
                        op0=ALU.mult, op1=ALU.mult), reads=[hk[kc], ("fb", 0), ("coef", l, ai)], writes=[tk])
                    P.op("dve", lambda e, kc=kc, t=t: e.tensor_scalar_add(
                        out=u[:, kc, u0:u0 + n], in0=t[:, 0:n], scalar1=mv[:, bpart * 8 + kc, U.v:U.v + 1]),
                        reads=[tk, ("modT", l, bpart)], writes=[("RU", kc)])
                    yield

        ybank = {"i": 0}

        def nextbank(nb=6):
            b = ybank["i"] % nb
            ybank["i"] = b + 1
            return b

        deferred_tail = []

        def flush_tail(upto=3):
            while deferred_tail and deferred_tail[0][0] <= upto:
                deferred_tail.pop(0)[1]()

        def proj_post_gen(U, l, gi, nk, wsrc, rhs_fn, rkeys):
            cf = cfs[l]
            steps = [(ct, m) for ct in range(2) for m in range(KC)]
            banks = {}
            ybank["i"] = 0

            def A(ct, m):
                si = load_w(wsrc(m), nk * 128)
                w3 = wv3(si, 128, nk)
                b = nextbank()
                banks[(ct, m)] = b

                nsplit = nk - 3 if (ct == 0 and m == 0 and nk > 8) else nk

                def f(e, k0=0, k1=nsplit):
                    for k in range(k0, k1):
                        ins = e.matmul(bank(b), w3[:, k, :], rhs_fn(k, ct), start=(k == 0), stop=(k == nk - 1))
                    return ins
                rk = lambda kk: rkeys[kk](ct) if callable(rkeys[kk]) else rkeys[kk]
                flat = lambda lo, hi: [key for kk in range(lo, hi) for key in rk(kk)]
                P.op("pe", f, reads=[("w", si)] + flat(0, nsplit), writes=[("ps", b)])
                if nsplit < nk:
                    P.op("pe", lambda e: f(e, nsplit, nk), reads=[("w", si)] + flat(nsplit, nk), writes=[("ps", b)])

            def ssmm(mm):
                P.op("pe", lambda e: e.matmul(bank(6), ones, SQ[2 + mm % 2][:, :], start=(mm == 0), stop=(mm == KC - 1)),
                     reads=[("sq", 2 + mm % 2), "ones"], writes=[("ps", 6)])

            def B(ct, m):
                b = banks[(ct, m)]
                P.op("act", lambda e: e.activation(out=ybuf[:, m, 0:512], in_=bank(b), func=AF.Copy),
                     reads=[("ps", b)], writes=[("RY", m)])
                s_ = SQ[2 + m % 2]
                P.op("act", lambda e: e.activation(out=s_[:, :], in_=bank(b), func=AF.Square),
                     reads=[("ps", b)], writes=[("sq", 2 + m % 2)])

            def tail(ct):
                hc0 = U.h0 + ct * 512
                make_rstd(6, 512, D, FB[3], ("fb", 3))

                def pair(p):
                    for m in (2 * p, 2 * p + 1):
                        hk = ("h", m, hc0 // 512)
                        P.op("dve", lambda e, m=m: e.scalar_tensor_tensor(
                            out=ybuf[:, m, 0:512], in0=ybuf[:, m, 0:512], scalar=cf[:, gi, m, U.v:U.v + 1], in1=FB[3][:, :],
                            op0=ALU.mult, op1=ALU.mult), reads=[("RY", m), ("fb", 3), ("coef", l, gi)], writes=[("RY", m)])
                        P.op("dve", lambda e, m=m: e.tensor_tensor(
                            out=h[:, m, hc0:hc0 + 512], in0=h[:, m, hc0:hc0 + 512], in1=ybuf[:, m, 0:512], op=ALU.add),
                            reads=[("RY", m), hk], writes=[hk])
                if ct == 0:
                    for p in range(4):
                        pair(p)
                else:
                    for p in range(4):
                        deferred_tail.append((p, lambda p=p: pair(p)))

            A(*steps[0])
            for i, (ct, m) in enumerate(steps):
                if i + 1 < len(steps):
                    A(*steps[i + 1])
                B(ct, m)
                if m > 0:
                    ssmm(m - 1)
                yield
                if m == KC - 1:
                    ssmm(m)
                    tail(ct)

        def padded_mm(T0, U, lhs_fn, si, split=False):
            def f(e, tiles=((0, 512), (512, 512), (1024, U.nh))):
                ins = None
                for (c0, n) in tiles:
                    for kc in range(KC):
                        ins = e.matmul(ps[:, T0 * 512 + c0:T0 * 512 + c0 + n], lhs_fn(kc), u[:, kc, c0:c0 + n],
                                       start=(kc == 0), stop=(kc == KC - 1))
                return ins
            if not split:
                P.op("pe", f, reads=[("w", si)] + ALLRU, writes=[("ps", T0), ("ps", T0 + 1), ("ps", T0 + 2)])
            else:
                P.op("pe", lambda e: f(e, ((0, 512),)), reads=[("w", si)] + ALLRU, writes=[("ps", T0)])
                P.op("pe", lambda e: f(e, ((512, 512), (1024, U.nh))), reads=[("w", si)] + ALLRU, writes=[("ps", T0 + 1), ("ps", T0 + 2)])

        def tkeys(T0):
            return [("ps", T0), ("ps", T0 + 1), ("ps", T0 + 2)]

        def tflat(T0):
            return ps[:, T0 * 512:T0 * 512 + 1536]

        def conv_in_gen(U):
            if U is S0:
                save_halo()
            switch_view()
            rot = {"i": 0}

            def nextT():
                t = rot["i"]
                rot["i"] = 1 - t
                return 3 * t
            ck = lambda j, i: par[:, P_CK + j * 8 + i:P_CK + j * 8 + i + 1]
            csb = ev[1]
            r = ev[0]
            W = 1024 + U.nh
            for i in range(8):
                si = load_w(win[i], KC * 384)
                w3 = wv3(si, 384)
                Tc = nextT()
                padded_mm(Tc, U, lambda kc, w3=w3: w3[:, kc, 128:256], si)
                flush_tail()
                P.op("act", lambda e, Tc=Tc: e.activation(out=csb[:, 0:W], in_=tflat(Tc)[:, 0:W], func=AF.Copy),
                     reads=tkeys(Tc), writes=EVK[1])
                Tx = nextT()
                padded_mm(Tx, U, lambda kc, w3=w3: w3[:, kc, 256:384], si)
                P.op("dve", lambda e, Tx=Tx: e.tensor_tensor(out=csb[:, 0:W], in0=csb[:, 0:W], in1=tflat(Tx)[:, 0:W], op=ALU.mult),
                     reads=tkeys(Tx) + EVK[1], writes=EVK[1])
                Tb = nextT()
                padded_mm(Tb, U, lambda kc, w3=w3: w3[:, kc, 0:128], si)
                rc = compact(r, U)
                P.op("act", lambda e, i=i: e.activation(out=rc, in_=vcols(csb, U, 0), func=AF.Copy, scale=ck(1, i)),
                     reads=EVK[1] + ["par"], writes=EVK[0])
                P.op("dve", lambda e, i=i: e.scalar_tensor_tensor(out=rc, in0=vcols(csb, U, -1), scalar=ck(0, i), in1=rc,
                                                                  op0=ALU.mult, op1=ALU.add), reads=EVK[1] + EVK[0] + ["par"], writes=EVK[0])
                P.op("dve", lambda e, i=i: e.scalar_tensor_tensor(out=rc, in0=vcols(csb, U, 1), scalar=ck(2, i), in1=rc,
                                                                  op0=ALU.mult, op1=ALU.add), reads=EVK[1] + EVK[0] + ["par"], writes=EVK[0])
                P.op("dve", lambda e, i=i, Tb=Tb: e.tensor_tensor(out=compact(a3[:, i, :], U), in0=rc, in1=vcols(tflat(Tb), U, 0), op=ALU.mult),
                     reads=EVK[0] + tkeys(Tb), writes=[("A", i)])
                if U is S0 and i == 3:
                    load_x(2, after=[("A", 3)])
                yield

        def conv_out_gen(U):
            return proj_post_gen(U, 0, 1, KC, lambda m: wout[m],
                                 lambda k, ct: a3[:, k, ct * 512:(ct + 1) * 512], [[("A", k)] for k in range(KC)])

        def ffn_in_gen(U, l):
            if U is S0:
                save_halo()
            switch_view()
            fck = lambda j, s, c: par[:, P_FCK + l * 132 + j * 44 + s * 22 + c: P_FCK + l * 132 + j * 44 + s * 22 + c + 1]
            for c in range(NCF):
                si = load_w(wup[l, c], KC * 256)
                w3 = wv3(si, 256)
                eg = 2 * (c % 2)
                for s in range(2):
                    T0 = 3 * s
                    padded_mm(T0, U, lambda kc, w3=w3, s=s: w3[:, kc, s * 128:(s + 1) * 128], si, split=U.sample)
                    flush_tail(eg + s)
                    evk = EVK[eg + s]
                    rc = compact(ev[eg + s], U)
                    zf = tflat(T0)
                    if U.sample:
                        evb = ev[eg + s]
                        P.op("act", lambda e, evb=evb, zf=zf, s=s, c=c: e.activation(out=evb[:, 0:511], in_=zf[:, 1:512], func=AF.Copy, scale=fck(1, s, c)),
                             reads=[("ps", T0), "par"], writes=evk)
                        P.op("act", lambda e, evb=evb, zf=zf, s=s, c=c: e.activation(out=evb[:, 511:1024], in_=zf[:, 512:1025], func=AF.Copy, scale=fck(1, s, c)),
                             reads=[("ps", T0 + 1), ("ps", T0 + 2), "par"], writes=evk)
                    else:
                        P.op("act", lambda e, rc=rc, zf=zf, s=s, c=c: e.activation(out=rc, in_=vcols(zf, U, 0), func=AF.Copy, scale=fck(1, s, c)),
                             reads=tkeys(T0) + ["par"], writes=evk)
                    if s == 1:
                        P.op("act", lambda e: e.activation(out=ev[eg][:, 0:1024], in_=ev[eg][:, 0:1024], func=AF.Silu),
                             reads=EVK[eg], writes=EVK[eg])
                    P.op("dve", lambda e, rc=rc, zf=zf, s=s, c=c: e.scalar_tensor_tensor(
                        out=rc, in0=vcols(zf, U, -1), scalar=fck(0, s, c), in1=rc, op0=ALU.mult, op1=ALU.add),
                        reads=tkeys(T0) + evk + ["par"], writes=evk)
                    P.op("dve", lambda e, rc=rc, zf=zf, s=s, c=c: e.scalar_tensor_tensor(
                        out=rc, in0=vcols(zf, U, 1), scalar=fck(2, s, c), in1=rc, op0=ALU.mult, op1=ALU.add),
                        reads=tkeys(T0) + evk + ["par"], writes=evk)
                    if s == 1:
                        P.op("dve", lambda e, c=c: e.tensor_tensor(out=a3[:, c, :], in0=ev[eg][:, 0:1024], in1=ev[eg + 1][:, 0:1024], op=ALU.mult),
                             reads=EVK[eg] + EVK[eg + 1], writes=[("A", c)])
                yield

        def ffn_out_gen(U, l):
            return proj_post_gen(U, l, 3, NCF, lambda m: wdn[l, m],
                                 lambda k, ct: a3[:, k, ct * 512:(ct + 1) * 512], [[("A", k)] for k in range(NCF)])

        qb = a3[:, 0:8, :]
        kt = ra_t[:, 8 * 1024:8 * 1024 + 2 * NKEY].rearrange("p (j s) -> p j s", j=2)
        vb = ra_t[:, 13 * 1024:13 * 1024 + 20 * 256].rearrange("p (c n) -> p c n", n=256)
        tab = ra_t[:, 18 * 1024:22 * 1024].bitcast(F32).rearrange("p (a t) -> p a t", a=2)
        KA = [("A", c) for c in range(8, 13)]
        VA = [("A", c) for c in range(13, 18)]
        TA = [("A", c) for c in range(18, 22)]
        att_scale = float(128 ** -0.5)
        qg = par[:, P_QG:P_QG + 1]
        kg = par[:, P_KG:P_KG + 1]

        def attn_setup():
            P.dma("pool", kt[:, :, 0:PAST], ckT, writes=["KT"] + KA)
            P.dma("pool", vb[:, 0:4, :], cvl, writes=["V"] + VA)

        def qk_chain(U, si, w3, ct, slot, dst, dkeys, gain, stage_out):
            b = nextbank(4)
            sqb, sqk = SQ[slot], ("sq", slot)
            X, Xk = evq[2 * slot], EVQK[2 * slot]
            Y, Yk = evq[2 * slot + 1], EVQK[2 * slot + 1]
            ssb = 4 + slot
            rtb = b

            def f(e):
                if U.sample:
                    for (u0, n, off) in U.vt[ct]:
                        for kc in range(KC):
                            ins = e.matmul(bank(b, n, off), w3[:, kc, :], u[:, kc, u0:u0 + n], start=(kc == 0), stop=(kc == KC - 1))
                else:
                    o3 = bank(b).rearrange("p (a n) -> p a n", a=2)
                    for kc in range(KC):
                        r3 = u[:, kc, 0:UW].rearrange("p (i w) -> p i w", w=258)[:, 2 * ct:2 * ct + 2, 1:257]
                        ins = e.matmul(o3, w3[:, kc, :], r3, start=(kc == 0), stop=(kc == KC - 1))
                return ins
            P.op("pe", f, reads=[("w", si)] + ALLRU, writes=[("ps", b)])
            yield
            P.op("act", lambda e: e.activation(out=sqb[:, :], in_=bank(b), func=AF.Square), reads=[("ps", b)], writes=[sqk])
            P.op("pe", lambda e: e.matmul(bank(ssb), ones, sqb[:, :], start=True, stop=True), reads=[sqk, "ones"], writes=[("ps", ssb)])
            yield
            make_rstd(ssb, 512, 128, X, Xk)
            yield
            if not U.sample:
                if stage_out is None:
                    P.op("dve", lambda e: e.scalar_tensor_tensor(out=dst, in0=bank(b), scalar=gain, in1=X[:, :], op0=ALU.mult, op1=ALU.mult),
                         reads=[("ps", b), Xk, "par"], writes=dkeys)
                else:
                    P.op("dve", lambda e: e.scalar_tensor_tensor(out=Y[:, :], in0=bank(b), scalar=gain, in1=X[:, :], op0=ALU.mult, op1=ALU.mult),
                         reads=[("ps", b), Xk, "par"], writes=[Yk])
                    P.op("act", lambda e: e.activation(out=dst, in_=Y[:, :], func=AF.Copy), reads=[Yk], writes=dkeys)
                    P.dma("sp", stage_out, Y[:, :], reads=[Yk], is_output=True)
                yield
            else:
                P.op("dve", lambda e: e.scalar_tensor_tensor(out=Y[:, :], in0=bank(b), scalar=gain, in1=X[:, :], op0=ALU.mult, op1=ALU.mult),
                     reads=[("ps", b), Xk, "par"], writes=[Yk])
                P.op("pe", lambda e: e.matmul(bank(rtb), rrot_t[:], Y[:, :], start=True, stop=True), reads=[Yk, "rrot"], writes=[("ps", rtb)])
                yield
                P.op("dve", lambda e: e.tensor_tensor(out=X[:, :], in0=Y[:, :], in1=tab[:, 0, ct * 512:(ct + 1) * 512], op=ALU.mult),
                     reads=[Yk, "tab"] + TA, writes=[Xk])
                yield
                P.op("dve", lambda e: e.tensor_tensor(out=Y[:, :], in0=bank(rtb), in1=tab[:, 1, ct * 512:(ct + 1) * 512], op=ALU.mult),
                     reads=[("ps", rtb), "tab"] + TA, writes=[Yk])
                yield
                P.op("dve", lambda e: e.tensor_tensor(out=dst, in0=Y[:, :], in1=X[:, :], op=ALU.add), reads=[Yk, Xk], writes=dkeys)
                yield

        def v_gen(U):
            si = load_w(wv, KC * 256)
            w3 = wv3(si, 256)
            for tt in range(8):
                if U.sample:
                    u0 = 1 + tt * 128
                    vch = 4 + U.h0 // 128 + tt
                else:
                    u0 = 1 + 258 * (tt // 2) + 128 * (tt % 2)
                    vch = tt
                b = nextbank(4)

                def f(e, b=b, u0=u0):
                    for kc in range(KC):
                        ins = e.matmul(bank(b, 256), u[:, kc, u0:u0 + 128], w3[:, kc, :], start=(kc == 0), stop=(kc == KC - 1))
                    return ins
                P.op("pe", f, reads=[("w", si)] + ALLRU, writes=[("ps", b)])
                P.op("act", lambda e, b=b, vch=vch: e.activation(out=vb[:, vch, :], in_=bank(b, 256), func=AF.Copy),
                     reads=[("ps", b)], writes=["V"] + VA)
                if not U.sample:
                    t, tk = evq[tt % 2 * 2 + 1], EVQK[tt % 2 * 2 + 1]
                    P.op("act", lambda e, b=b, t=t: e.activation(out=t[:, 0:256], in_=bank(b, 256), func=AF.Copy), reads=[("ps", b)], writes=[tk])
                    P.dma("sp", v_out[tt * 128:(tt + 1) * 128, :], t[:, 0:256], reads=[tk], is_output=True)
                yield

        def attn_in_gen(U, kv, q):
            flush_tail()
            if U.sample:
                for a in range(2):
                    P.dma("sp", tab[:, a, :], rope[a, :, U.h0:U.h0 + 1024], writes=["tab"] + TA)
            chains = []
            heads = ([8, 9] if kv else []) + (list(range(8)) if q else [])
            kbase = (PAST + U.h0) if U.sample else 0
            active = []
            slot_free = [0, 1, 2, 3]
            if kv:
                yield from v_gen(U)
            for hd in heads:
                si = load_w(wqk[hd], KC * 128)
                w3 = wv3(si, 128)
                for ct in range(2):
                    if hd >= 8:
                        j = hd - 8
                        dst = kt[:, j, kbase + ct * 512:kbase + (ct + 1) * 512]
                        dkeys = ["QK"] + KA
                        so = None if U.sample else k_out[:, j, ct * 512:(ct + 1) * 512]
                        gain = kg
                    else:
                        dst = qb[:, hd, ct * 512:(ct + 1) * 512]
                        dkeys = [("A", hd), ("Aq", hd, ct * 512), ("Aq", hd, ct * 512 + 256)]
                        so = None
                        gain = qg
                    while not slot_free:
                        for (g_, s_) in list(active):
                            try:
                                next(g_)
                            except StopIteration:
                                active.remove((g_, s_))
                                slot_free.append(s_)
                        yield
                    s_ = slot_free.pop(0)
                    g_ = qk_chain(U, si, w3, ct, s_, dst, dkeys, gain, so)
                    next(g_)
                    active.append((g_, s_))
            while active:
                for (g_, s_) in list(active):
                    try:
                        next(g_)
                    except StopIteration:
                        active.remove((g_, s_))
                yield

        def attend_gen(U):
            grp = {"i": 0}
            sbk = {"i": 0}
            pti = {"i": 0}
            if U.sample:
                jobs = [(hd, 1, qt * 512, 512, [(c * 128, c) for c in range(20)]) for hd in range(8) for qt in range(2)]
            else:
                jobs = [(hp * 2, 2, sq * 256, 256, [(sq * 256 + c * 128, 2 * sq + c) for c in range(2)])
                        for sq in range(4) for hp in range(4)]
            pending_fin = []
            for (hd, nh, q0, nq, keys) in jobs:
                j = hd // 4
                g = grp["i"]
                grp["i"] = 1 - g
                ob, db = 3 + g, 5 + g
                pend = []
                NQ = nh * nq
                qkeys = [("Aq", hd + a, q0 + c0) for a in range(nh) for c0 in ((0, 256) if nq == 512 else (0,))]
                q_ap = qb[:, hd:hd + nh, q0:q0 + nq]

                def v3(ap2d):
                    return ap2d.rearrange("p (a n) -> p a n", a=nh)

                def pv(item, first, last):
                    (pi, vch) = item
                    P.op("pe", lambda e: e.matmul(bank(ob, NQ), vb[:, vch, j * 128:(j + 1) * 128], pt[pi][:, 0:NQ], start=first, stop=last),
                         reads=[PTK[pi], "V"] + VA, writes=[("ps", ob)])
                    P.op("pe", lambda e: e.matmul(bank(db, NQ), ones, pt[pi][:, 0:NQ], start=first, stop=last),
                         reads=[PTK[pi], "ones"], writes=[("ps", db)])
                done = 0
                for ki, (k0, vch) in enumerate(keys):
                    sbank = sbk["i"]
                    sbk["i"] = (sbank + 1) % 3
                    pi = pti["i"]
                    pti["i"] = (pi + 1) % 4
                    P.op("pe", lambda e: e.matmul(v3(bank(sbank, NQ)), kt[:, j, k0:k0 + 128], q_ap, start=True, stop=True),
                         reads=["KT", "QK"] + qkeys + KA, writes=[("ps", sbank)])
                    P.op("act", lambda e: e.activation(out=pt[pi][:, 0:NQ], in_=bank(sbank, NQ), func=AF.Exp, scale=att_scale),
                         reads=[("ps", sbank)], writes=[PTK[pi]])
                    pend.append((pi, vch))
                    if len(pend) > 2:
                        pv(pend.pop(0), done == 0, False)
                        done += 1
                    if ki == min(1, len(keys) - 1) and pending_fin:
                        pending_fin.pop(0)()
                    yield
                while pend:
                    it = pend.pop(0)
                    pv(it, done == 0, len(pend) == 0)
                    done += 1
                def fin(ob=ob, db=db, NQ=NQ, q_ap=q_ap, qkeys=qkeys, v3=v3):
                    if U.sample:
                        P.op("dve", lambda e: e.reciprocal(out=FB[3][:, 0:NQ], in_=bank(db, NQ)), reads=[("ps", db)], writes=[("fb", 3)])
                    else:
                        P.op("act", lambda e: e.activation(out=FB[3][:, 0:NQ], in_=bank(db, NQ), func=AF.Ln), reads=[("ps", db)], writes=[("fb", 3)])
                        P.op("act", lambda e: e.activation(out=FB[3][:, 0:NQ], in_=FB[3][:, 0:NQ], func=AF.Exp, scale=-1.0), reads=[("fb", 3)], writes=[("fb", 3)])
                    P.op("dve", lambda e: e.tensor_tensor(out=q_ap, in0=v3(bank(ob, NQ)), in1=v3(FB[3][:, 0:NQ]), op=ALU.mult),
                         reads=[("ps", ob), ("fb", 3)], writes=qkeys)
                pending_fin.append(fin)
                yield
            while pending_fin:
                pending_fin.pop(0)()

        def attn_out_gen(U):
            yield from attend_gen(U)
            yield from proj_post_gen(U, 1, 1, KC, lambda m: wo[m],
                                     lambda k, ct: qb[:, k, ct * 512:(ct + 1) * 512],
                                     [(lambda ct, k=k: [("A", k), ("Aq", k, ct * 512), ("Aq", k, ct * 512 + 256)]) for k in range(KC)])

        class Stage:
            def __init__(self, norm, inn, out, per=0.2):
                self.norm, self.inn, self.out, self.per = norm, inn, out, per
                self.norm_done = False

        def chain(*gens):
            for g_ in gens:
                yield from g_

        stages = []
        for U in units:
            inn = (lambda U=U: conv_in_gen(U))
            if U is S0:
                inn = (lambda U=U: weave_gen(conv_in_gen(U), mod_gen(0, [2, 3]), 0.8))
            if U is S1:
                inn = (lambda U=U: weave_gen(conv_in_gen(U), mod_gen(0, [4]), 1.6))
            if U is PU:
                inn = (lambda U=U: weave_gen(conv_in_gen(U), mod_gen(0, [5]), 1.6))
            stages.append(Stage(lambda U=U: norm_gen(U, 0, 0, 0, skip_first_stats=(U is S0)), inn, lambda U=U: conv_out_gen(U)))
        for U in units:
            inn = (lambda U=U: ffn_in_gen(U, 0))
            if U is S1:
                inn = (lambda U=U: weave_gen(ffn_in_gen(U, 0), mod_gen(1, [0, 1, 2]), 1.4))
            if U is PU:
                inn = (lambda U=U: weave_gen(ffn_in_gen(U, 0), mod_gen(1, [3, 4, 5]), 1.4))
            stages.append(Stage(lambda U=U: norm_gen(U, 0, 2, 3), inn, lambda U=U: ffn_out_gen(U, 0)))
        stages.append(Stage(lambda: norm_gen(S0, 1, 0, 0, halo=False), lambda: chain_setup(attn_in_gen(S0, True, False)), None))
        stages.append(Stage(lambda: norm_gen(S1, 1, 0, 0, halo=False), lambda: attn_in_gen(S1, True, True), lambda: attn_out_gen(S1), per=6.0))
        stages.append(Stage(lambda: norm_gen(S0, 1, 0, 0, halo=False), lambda: attn_in_gen(S0, False, True), lambda: attn_out_gen(S0), per=4.4))
        stages.append(Stage(lambda: norm_gen(PU, 1, 0, 0, halo=False), lambda: chain(attn_in_gen(PU, True, True), attend_gen(PU)),
                            lambda: proj_post_gen(PU, 1, 1, KC, lambda m: wo[m], lambda k, ct: qb[:, k, ct * 512:(ct + 1) * 512],
                                                  [(lambda ct, k=k: [("A", k), ("Aq", k, ct * 512), ("Aq", k, ct * 512 + 256)]) for k in range(KC)])))
        for U in units:
            stages.append(Stage(lambda U=U: norm_gen(U, 1, 2, 3), lambda U=U: ffn_in_gen(U, 1), lambda U=U: ffn_out_gen(U, 1)))

        def weave_gen(main, side, per):
            side = iter(side)
            credit = 0.0
            s_alive = True
            for _ in main:
                credit += 1.0
                while s_alive and credit >= per:
                    credit -= per
                    try:
                        next(side)
                    except StopIteration:
                        s_alive = False
                yield
            if s_alive:
                for _ in side:
                    yield

        def chain_setup(g_):
            yield from g_
            attn_setup()

        run(norm_gen(S0, 0, 0, 0, stats_only_first=True))
        run(mod_gen(0, [0, 1]))
        load_x(1, after=[("coef", 0, 0)])
        for i, st in enumerate(stages):
            if not st.norm_done:
                flush_tail()
                run(st.norm())
            run(st.inn())
            nxt = stages[i + 1] if i + 1 < len(stages) else None
            if st.out is not None:
                if nxt is not None:
                    weave(st.out(), nxt.norm(), st.per)
                    nxt.norm_done = True
                else:
                    run(st.out())
        flush_tail()
        for ui in range(3):
            for kc in range(KC):
                P.dma("sp", h_out[:, kc, ui * 1024:(ui + 1) * 1024], h[:, kc, ui * 1024:(ui + 1) * 1024],
                      reads=[("h", kc, 2 * ui), ("h", kc, 2 * ui + 1)], is_output=True)
        P.finish()
    return nc


def _fm(w):
    return w.reshape(KC, 128, w.shape[1]).transpose(1, 0, 2)


def _host_layout(inp):
    f = lambda a: np.ascontiguousarray(a, dtype=np.float32)
    shared = {}
    mod_w = inp["mod_w"]
    shared["modw"] = f(np.stack([np.stack([_fm(mod_w[l][:, blk * 256:(blk + 1) * 256]).reshape(128, KC * 256)
                                           for blk in range(24)]) for l in range(2)]))
    w_in = inp["conv_w_in"][0]
    shared["win"] = f(np.stack([np.concatenate([_fm(w_in[:, s * 1024 + i * 128: s * 1024 + (i + 1) * 128]) for s in range(3)], axis=2)
                                .reshape(128, KC * 384) for i in range(8)]))
    w_o1 = inp["conv_w_out"][0]
    shared["wout"] = f(np.stack([_fm(w_o1[:, m * 128:(m + 1) * 128]).reshape(128, KC * 128) for m in range(8)]))
    wqkv = inp["attn_w_qkv"][0]
    shared["wqk"] = f(np.stack([_fm(wqkv[:, hd * 128:(hd + 1) * 128]).reshape(128, KC * 128) for hd in range(10)]))
    shared["wv"] = f(_fm(wqkv[:, 1280:1536]).reshape(128, KC * 256))
    w_o2 = inp["attn_w_o"][0]
    shared["wo"] = f(np.stack([_fm(w_o2[:, m * 128:(m + 1) * 128]).reshape(128, KC * 128) for m in range(8)]))
    wup = inp["ffn_w_up"]
    shared["wup"] = f(np.stack([np.stack([np.concatenate([_fm(wup[l][:, s * DFF + c * 128: s * DFF + (c + 1) * 128]) for s in range(2)], axis=2)
                                          .reshape(128, KC * 256) for c in range(NCF)]) for l in range(2)]))
    wdn = inp["ffn_w_down"]
    shared["wdn"] = f(np.stack([np.stack([wdn[l][:, m * 128:(m + 1) * 128].reshape(NCF, 128, 128).transpose(1, 0, 2).reshape(128, NCF * 128)
                                          for m in range(8)]) for l in range(2)]))
    t = np.arange(TS)
    row = (t // 64).astype(np.float32)
    col = (t % 64).astype(np.float32)
    inv = (10000.0 ** (-np.arange(0, 64, 2, dtype=np.float32) / 64)).astype(np.float32)
    cos = np.zeros((128, TS), np.float32)
    sin = np.zeros((128, TS), np.float32)
    for d in range(128):
        pos = row if d < 64 else col
        ang = (pos * inv[d % 32]).astype(np.float32)
        cos[d] = np.cos(ang)
        sin[d] = np.sin(ang)
    shared["rope"] = f(np.stack([cos, sin]))
    R = np.zeros((128, 128), np.float32)
    for d in range(128):
        blk = (d // 64) * 64
        dd = d % 64
        if dd < 32:
            R[blk + dd + 32, d] = -1.0
        else:
            R[blk + dd - 32, d] = 1.0
    shared["rrot"] = R
    par = np.zeros((128, P_N), np.float32)
    gains = [inp["norm_mix_pre"], inp["norm_mix_post"], inp["norm_ffn_pre"], inp["norm_ffn_post"]]
    for l in range(2):
        for wi, g in enumerate(gains):
            par[:, P_GAIN + l * 32 + wi * 8: P_GAIN + l * 32 + wi * 8 + 8] = g[l].reshape(KC, 128).T
        par[:, P_MODB + l * 48: P_MODB + (l + 1) * 48] = inp["mod_b"][l].reshape(48, 128).T
        fc = inp["ffn_conv"][l]
        for j in range(3):
            par[:, P_FCK + l * 132 + j * 44: P_FCK + l * 132 + (j + 1) * 44] = fc[j].reshape(44, 128).T
    ck = inp["conv_k"][0]
    for j in range(3):
        par[:, P_CK + j * 8: P_CK + (j + 1) * 8] = ck[j].reshape(8, 128).T
    par[:, P_QG] = inp["attn_q_gain"][0]
    par[:, P_KG] = inp["attn_k_gain"][0]
    par[:, P_EPS] = EPS
    cctx = inp["c_ctx"].reshape(KC, 128).T
    in_maps = []
    for b in range(N_CORES):
        m = dict(shared)
        p = par.copy()
        cv = np.stack([inp["c"][b].reshape(KC, 128).T, cctx], axis=2)
        p[:, P_CVEC:P_CVEC + 16] = cv.reshape(128, 16)
        m["params"] = p
        xs = inp["x_sample"][b]
        xp = inp["x_prompt"][4 * b:4 * b + 4].reshape(TP, D)
        xa = np.concatenate([xs, xp], axis=0)
        m["x_in"] = f(xa.reshape(TT, KC, 128).transpose(2, 1, 0))
        m["ckT"] = f(inp["cache_k"][b, 0].transpose(2, 1, 0))
        m["cvl"] = f(inp["cache_v"][b, 0].reshape(4, 128, 256).transpose(1, 0, 2))
        in_maps.append(m)
    return in_maps


_NC_CACHE = {}


def kernel(**inputs):
    inp = {k: np.asarray(v) for k, v in inputs.items()}
    in_maps = _host_layout(inp)
    if "nc" not in _NC_CACHE:
        _NC_CACHE["nc"] = build_program()
    nc = _NC_CACHE["nc"]
    res = run_bass_kernel_spmd(nc, in_maps, core_ids=list(range(N_CORES)))
    y_prompt = np.zeros((32, 256, D), np.float32)
    y_sample = np.zeros((8, TS, D), np.float32)
    new_k = np.zeros((32, 1, 256, 2, 128), np.float32)
    new_v = np.zeros((32, 1, 256, 2, 128), np.float32)
    for b in range(N_CORES):
        r = res.results[b]
        ho = np.asarray(r["h_out"]).transpose(2, 1, 0).reshape(TT, D)
        y_sample[b] = ho[:TS]
        y_prompt[4 * b:4 * b + 4] = ho[TS:].reshape(4, 256, D)
        ko = np.asarray(r["k_out"]).transpose(2, 1, 0)
        new_k[4 * b:4 * b + 4, 0] = ko.reshape(4, 256, 2, 128)
        vo = np.asarray(r["v_out"]).reshape(4, 256, 2, 128)
        new_v[4 * b:4 * b + 4, 0] = vo
    return (y_prompt, y_sample, new_k, new_v)
```
